# Optimizing a Trainium2 kernel written in Bass

```python
import jax, jax.numpy as jnp
from jax import lax
import numpy as np

D_MODEL = 1024
BATCH = 8
SEQ = 4096
DEPTH = 1

CTX_LEN = 256
GRID_W = 64

GLA_HEADS = 4
GLA_DK = 128
GLA_DV = 256
GLA_KW = GLA_HEADS * GLA_DK
GLA_VW = GLA_HEADS * GLA_DV
GATE_RANK = 16
GATE_NORM = 16.0
CHUNK = 64
ROPE_BASE = 10000.0

POOL_WINDOWS = (2, 4, 8, 16)
POOL_GROUPS = 4
POOL_W = D_MODEL
POOL_GW = POOL_W // POOL_GROUPS

EPS = 1e-6

IN_SPLITS = (GLA_KW, GLA_KW, GLA_VW, GLA_VW, GATE_RANK, GATE_RANK, POOL_W, POOL_W, D_MODEL, D_MODEL)
D_IN = 2 * GLA_KW + 2 * GLA_VW + 2 * GATE_RANK + 2 * POOL_W + 2 * D_MODEL

kernel_name = "hybrid_gla_pool_prefix_dit"


def rmsnorm(x, g):
    xf = x.astype(jnp.float32)
    y = xf * lax.rsqrt(jnp.mean(xf * xf, axis=-1, keepdims=True) + EPS)
    return (y * g.astype(jnp.float32)).astype(x.dtype)


def rope_2d(t, row, col):
    half = GLA_DK // 2
    nf = half // 2
    freqs = ROPE_BASE ** (-jnp.arange(nf, dtype=jnp.float32) / nf)

    def rot(xp, pos):
        ang = pos[:, None] * freqs
        cos = jnp.cos(ang)[None, :, None, :]
        sin = jnp.sin(ang)[None, :, None, :]
        xa, xb = xp[..., :nf], xp[..., nf:]
        return jnp.concatenate([xa * cos - xb * sin, xa * sin + xb * cos], axis=-1)

    return jnp.concatenate([rot(t[..., :half], row), rot(t[..., half:], col)], axis=-1)


def gla_chunked(q, k, v, log_a, s0, with_output):
    B, L, H, DK = q.shape
    n = L // CHUNK

    def to_chunks(t):
        return t.astype(jnp.float32).reshape(B, n, CHUNK, H, t.shape[-1]).transpose(1, 0, 3, 2, 4)

    tri = jnp.tril(jnp.ones((CHUNK, CHUNK), dtype=bool))

    def step(S, inp):
        qc, kc, vc, ac = inp
        b = jnp.cumsum(ac, axis=2)
        bC = b[:, :, -1:, :]
        S_new = jnp.exp(bC[:, :, 0, :, None]) * S + jnp.einsum(
            'bhjd,bhjv->bhdv', kc * jnp.exp(bC - b), vc)
        if not with_output:
            return S_new, None
        diff = b[:, :, :, None, :] - b[:, :, None, :, :]
        decay = jnp.exp(jnp.where(tri[:, :, None], diff, -jnp.inf))
        scores = jnp.einsum('bhid,bhjd,bhijd->bhij', qc, kc, decay)
        o = jnp.einsum('bhij,bhjv->bhiv', scores, vc) + jnp.einsum(
            'bhid,bhdv->bhiv', qc * jnp.exp(b), S)
        return S_new, o

    S, o = lax.scan(step, s0, (to_chunks(q), to_chunks(k), to_chunks(v), to_chunks(log_a)))
    if not with_output:
        return None, S
    o = o.transpose(1, 0, 3, 2, 4).reshape(B, L, H, v.shape[-1])
    return o, S


def gla_bidir(q, k, v, la_f, la_b, s0_f, s0_b, with_output):
    flip = lambda t: t[:, ::-1]
    o_f, s_f = gla_chunked(q, k, v, la_f, s0_f, with_output)
    o_b, s_b = gla_chunked(flip(q), flip(k), flip(v), flip(la_b), s0_b, with_output)
    o = o_f + flip(o_b) if with_output else None
    return o, s_f, s_b


def gla_log_decay(lowrank, w_up, b):
    B, L, _ = lowrank.shape
    gk = (lowrank @ w_up + b).astype(jnp.float32)
    return (jax.nn.log_sigmoid(gk) / GATE_NORM).reshape(B, L, GLA_HEADS, GLA_DK)


def mixer_inputs(h, w_in, w_up_f, b_f, w_up_b, b_b, row, col):
    B, L, _ = h.shape
    p = h @ w_in
    idx = np.cumsum(IN_SPLITS)[:-1].tolist()
    pq, pk, pv, zA, lr_f, lr_b, uB, zB, gA, gB = jnp.split(p, idx, axis=-1)
    q = pq.reshape(B, L, GLA_HEADS, GLA_DK) * (GLA_DK ** -0.5)
    k = pk.reshape(B, L, GLA_HEADS, GLA_DK)
    if row is not None:
        q = rope_2d(q, row, col)
        k = rope_2d(k, row, col)
    v = pv.reshape(B, L, GLA_HEADS, GLA_DV)
    la_f = gla_log_decay(lr_f, w_up_f, b_f)
    la_b = gla_log_decay(lr_b, w_up_b, b_b)
    return q, k, v, la_f, la_b, zA, uB, zB, gA, gB


def multiscale_pool(u, w_pool, pool_scale):
    B, L, _ = u.shape
    uf = u.astype(jnp.float32)
    cs = jnp.concatenate([jnp.zeros_like(uf[:, :1]), jnp.cumsum(uf, axis=1)], axis=1)
    t = jnp.arange(L)
    parts = []
    for g, w in enumerate(POOL_WINDOWS):
        lo = jnp.clip(t - w // 2, 0, L - 1)
        hi = jnp.clip(t + (w - 1 - w // 2), 0, L - 1)
        sl = slice(g * POOL_GW, (g + 1) * POOL_GW)
        csg = cs[..., sl]
        cnt = (hi - lo + 1).astype(jnp.float32)[:, None]
        parts.append((csg[:, hi + 1] - csg[:, lo]) / cnt - uf[..., sl])
    d = jnp.stack(parts, axis=2)
    y = jnp.einsum('blgc,gcd->blgd', d, w_pool.astype(jnp.float32)).reshape(B, L, POOL_W)
    return (y * pool_scale.astype(jnp.float32)).astype(u.dtype)


def branch_out(o, zA, uB, zB, gA, gB, gla_norm_g, w_pool, pool_scale, w_branch_a, w_branch_b, w_out):
    B, L, _ = zA.shape
    o_n = rmsnorm(o, gla_norm_g).astype(zA.dtype).reshape(B, L, GLA_VW)
    yA = (o_n * jax.nn.silu(zA)) @ w_branch_a
    yB = (multiscale_pool(uB, w_pool, pool_scale) * jax.nn.silu(zB)) @ w_branch_b
    merged = jax.nn.sigmoid(gA) * yA + jax.nn.sigmoid(gB) * yB
    return merged @ w_out


def setup_inputs(seed: int = 0) -> dict:
    key = jax.random.key(seed)
    ks = jax.random.split(key, 24)
    f32 = jnp.float32
    nrm = lambda k, s, sc: jax.random.normal(k, s, f32) * sc
    return {
        "x": nrm(ks[0], (BATCH, SEQ, D_MODEL), 1.0),
        "c": nrm(ks[1], (BATCH, D_MODEL), 1.0),
        "ctx": nrm(ks[2], (BATCH, CTX_LEN, D_MODEL), 1.0),
        "c_ctx": nrm(ks[3], (D_MODEL,), 1.0),
        "w_mod": nrm(ks[4], (DEPTH, D_MODEL, 3 * D_MODEL), D_MODEL ** -0.5),
        "b_mod": nrm(ks[5], (DEPTH, 3 * D_MODEL), 0.01),
        "norm_g": 1.0 + nrm(ks[6], (DEPTH, D_MODEL), 0.1),
        "w_in": nrm(ks[7], (DEPTH, D_MODEL, D_IN), D_MODEL ** -0.5),
        "w_gate_up_f": nrm(ks[8], (DEPTH, GATE_RANK, GLA_KW), GATE_RANK ** -0.5),
        "b_gate_f": jax.random.uniform(ks[9], (DEPTH, GLA_KW), f32, 1.0, 5.0),
        "w_gate_up_b": nrm(ks[10], (DEPTH, GATE_RANK, GLA_KW), GATE_RANK ** -0.5),
        "b_gate_b": jax.random.uniform(ks[11], (DEPTH, GLA_KW), f32, 1.0, 5.0),
        "gla_norm_g": 1.0 + nrm(ks[12], (DEPTH, GLA_DV), 0.1),
        "w_pool": nrm(ks[13], (DEPTH, POOL_GROUPS, POOL_GW, POOL_GW), POOL_GW ** -0.5),
        "pool_scale": 1.0 + nrm(ks[14], (DEPTH, POOL_W), 0.1),
        "w_branch_a": nrm(ks[15], (DEPTH, GLA_VW, D_MODEL), GLA_VW ** -0.5),
        "w_branch_b": nrm(ks[16], (DEPTH, POOL_W, D_MODEL), POOL_W ** -0.5),
        "w_out": nrm(ks[17], (DEPTH, D_MODEL, D_MODEL), D_MODEL ** -0.5),
        "final_norm_g": 1.0 + nrm(ks[18], (D_MODEL,), 0.1),
    }


def reference(x, c, ctx, c_ctx, w_mod, b_mod, norm_g, w_in, w_gate_up_f, b_gate_f,
              w_gate_up_b, b_gate_b, gla_norm_g, w_pool, pool_scale, w_branch_a,
              w_branch_b, w_out, final_norm_g):
    B, L, _ = x.shape
    ROWS = L // GRID_W
    t = jnp.arange(ROWS * GRID_W)
    row = (t // GRID_W).astype(jnp.float32)
    col = (t % GRID_W).astype(jnp.float32)

    for l in range(DEPTH):
        last = l == DEPTH - 1
        shift_x, scale_x, gate_x = jnp.split(jax.nn.silu(c) @ w_mod[l] + b_mod[l], 3, axis=-1)
        shift_x, scale_x, gate_x = shift_x[:, None], scale_x[:, None], gate_x[:, None]
        shift_c, scale_c, gate_c = jnp.split(jax.nn.silu(c_ctx) @ w_mod[l] + b_mod[l], 3, axis=-1)

        hc = rmsnorm(ctx, norm_g[l]) * (1.0 + scale_c) + shift_c
        qc, kc, vc, laf_c, lab_c, zAc, uBc, zBc, gAc, gBc = mixer_inputs(
            hc, w_in[l], w_gate_up_f[l], b_gate_f[l], w_gate_up_b[l], b_gate_b[l], None, None)
        s_zero = jnp.zeros((B, GLA_HEADS, GLA_DK, GLA_DV), jnp.float32)
        oc, s_f, s_b = gla_bidir(qc, kc, vc, laf_c, lab_c, s_zero, s_zero, not last)

        hx = rmsnorm(x, norm_g[l]) * (1.0 + scale_x) + shift_x
        q, k, v, la_f, la_b, zA, uB, zB, gA, gB = mixer_inputs(
            hx, w_in[l], w_gate_up_f[l], b_gate_f[l], w_gate_up_b[l], b_gate_b[l], row, col)
        ox, _, _ = gla_bidir(q, k, v, la_f, la_b, s_f, s_b, True)
        y = branch_out(ox.astype(x.dtype), zA, uB, zB, gA, gB, gla_norm_g[l], w_pool[l],
                       pool_scale[l], w_branch_a[l], w_branch_b[l], w_out[l])
        x_new = x + gate_x * y

        if not last:
            yc = branch_out(oc.astype(ctx.dtype), zAc, uBc, zBc, gAc, gBc, gla_norm_g[l], w_pool[l],
                            pool_scale[l], w_branch_a[l], w_branch_b[l], w_out[l])
            ctx = ctx + gate_c * yc
        x = x_new

    return rmsnorm(x, final_norm_g)
```

```python
import contextlib
import math
import numpy as np
import concourse.bass as bass
import concourse.mybir as mybir
from concourse.bass_utils import run_bass_kernel_spmd

F32 = mybir.dt.float32
BF16 = mybir.dt.bfloat16
I32 = mybir.dt.int32
AF = mybir.ActivationFunctionType
ALU = mybir.AluOpType

D = 1024
L = 4096
LC = 256
C = 128
NCK = L // C
NCH = 2
NT = NCK // NCH
HR = NCH + 1
UR = NCH + 2
DIN = 7200
EPS = 1e-6
QSCALE = 128 ** -0.5
WINS = (2, 4, 8, 16)
GOFF = dict(q=0, k=512, v0=1024, v1=1536, zA0=2048, zA1=2560, u0=3104, u1=3616,
            zB0=4128, zB1=4640, gA0=5152, gA1=5664, gB0=6176, gB1=6688)
LROFF = dict(f=3072, b=3088)


class Prog:
    ENGS = ("pe", "dve", "act", "pool", "sp")

    def __init__(self, nc):
        self.nc = nc
        self.ops = {e: [] for e in self.ENGS}
        self.clock = {e: {} for e in self.ENGS}
        self.evclock = {}
        self.reg = {}
        self.dma_cnt = {}

    def _need(self, eng, ev, waits):
        key, val = ev
        if key == eng == "pe":
            return
        ck = self.clock[eng]
        if ck.get(key, 0) >= val:
            return
        waits.append(ev)
        if key != eng:
            for k, v in self.evclock.get(ev, {}).items():
                if ck.get(k, 0) < v:
                    ck[k] = v
        if ck.get(key, 0) < val:
            ck[key] = val

    @staticmethod
    def _psum(k):
        return isinstance(k, tuple) and k[0] in ("gen", "pair", "tpb")

    def op(self, eng, meth, *args, R=(), W=(), dma=None, **kw):
        nk = lambda k: ("tpb",) if (isinstance(k, tuple) and k[0] == "tpb") else k
        R = [nk(k) for k in R]
        W = [nk(k) for k in W]
        W = list(dict.fromkeys(W + [k for k in R if self._psum(k)]))
        R = [k for k in R if not self._psum(k)]
        waits = []
        idx = len(self.ops[eng])
        for r in R:
            st = self.reg.get(r)
            if st and st[0] is not None:
                self._need(eng, st[0], waits)
        for w in W:
            st = self.reg.get(w)
            if st:
                if st[0] is not None:
                    self._need(eng, st[0], waits)
                for ev in st[1]:
                    self._need(eng, ev, waits)
        if dma is None:
            myev = (eng, idx + 1)
            self.evclock[myev] = dict(self.clock[eng])
        else:
            n = self.dma_cnt.get(dma, 0) + 16
            self.dma_cnt[dma] = n
            myev = (("dma", dma), n)
            ck_ = dict(self.clock[eng])
            ck_.pop(eng, None)
            self.evclock[myev] = ck_
        self.ops[eng].append((waits, meth, args, kw, None if dma is None else myev))
        for r in R:
            st = self.reg.setdefault(r, [None, []])
            st[1].append(myev)
        for w in W:
            self.reg[w] = [myev, []]
        return myev

    def emit(self, final_waits=()):
        nc = self.nc
        with contextlib.ExitStack() as es:
            semh = {}
            for e in self.ENGS:
                semh[e] = es.enter_context(nc.semaphore("s_" + e))
            for i, k in enumerate(self.dma_cnt):
                semh[("dma", k)] = es.enter_context(nc.semaphore("d%d" % i))
            block = es.enter_context(nc.Block())
            remap = {}
            for e in self.ENGS:
                c = 0
                for i, o in enumerate(self.ops[e]):
                    if o[4] is None:
                        c += 1
                    remap[(e, i + 1)] = c

            def val(k, v):
                return v if isinstance(k, tuple) else remap[(k, v)]

            def run(e, eng):
                for (waits, meth, args, kw, dmaev) in self.ops[e]:
                    for (k, v) in waits:
                        eng.wait_ge(semh[k], val(k, v))
                    ins = getattr(eng, meth)(*args, **kw)
                    if dmaev is None:
                        ins.then_inc(semh[e], 1)
                    else:
                        ins.then_inc(semh[dmaev[0]], 16)
                if e == "pool":
                    for (k, v) in final_waits:
                        eng.wait_ge(semh[k], val(k, v))

            block.tensor(lambda eng: run("pe", eng))
            block.vector(lambda eng: run("dve", eng))
            block.scalar(lambda eng: run("act", eng))
            block.gpsimd(lambda eng: run("pool", eng))
            block.sync(lambda eng: run("sp", eng))


class _Stop(Exception):
    pass


def build_nc(stage=4, debug=False, sub=99):
    nc = bass.Bass("TRN2", target_bir_lowering=False)
    P = Prog(nc)

    def din(name, shape, dt=F32):
        return nc.dram_tensor(name, list(shape), dt, kind="ExternalInput").ap()

    def dint(name, shape, dt):
        return nc.dram_tensor(name, list(shape), dt, kind="Internal").ap()

    x_d = din("x", [L, D])
    ctx_d = din("ctx", [LC, D])
    cc_d = din("cc", [128, 8, 2])
    wmod_d = din("w_mod", [D, 3 * D])
    bmod_d = din("bmod_fm", [128, 24])
    bgate_d = din("bgate_bc", [128, D])
    normg_d = din("normg_fm", [128, 8])
    win_d = din("w_in", [D, DIN])
    wup_d = {"f": din("wup_f", [65, 512]), "b": din("wup_b", [65, 512])}
    glag_d = din("glag_fm", [128, 8])
    wpool_d = din("w_pool", [4, 256, 256])
    pscale_d = din("pscale_fm", [128, 8])
    wa_d = din("w_a", [D, D])
    wb_d = din("w_b", [D, D])
    wo_d = din("w_o", [D, D])
    fng_d = din("fng_bc", [128, D])
    out_d = nc.dram_tensor("out", [L, D], F32, kind="ExternalOutput").ap()

    wins = {g: dint("wins_" + g, [128, 8, 512], BF16) for g in GOFF}
    wlrs = {d: dint("wlrs_" + d, [128, 8, 16], BF16) for d in "fb"}
    wpools = dint("wpools", [128, 4, 2, 256], BF16)
    was = dint("was", [128, 8, D], BF16)
    wbs = dint("wbs", [128, 8, D], BF16)
    wos = dint("wos", [128, 8, D], BF16)
    sbscr = dint("sbscr", [NCK, 128, D], BF16)

    es = contextlib.ExitStack()
    with es:
        sb_total = [0]

        def sb(name, shape, dt=F32):
            n = 1
            for v_ in shape[1:]:
                n *= v_
            sb_total[0] += n * (2 if dt == BF16 else 4)
            return es.enter_context(nc.sbuf_tensor("sb_" + name, list(shape), dt))

        def ps(name, shape, dt=F32):
            return es.enter_context(nc.psum_tensor("ps_" + name, list(shape), dt))

        gen = [ps("gen%d" % i, [128, 512]) for i in range(4)]
        pair = [ps("pair%d" % i, [128, 1024]) for i in range(2)]
        gen_i = [0]
        pair_i = [0]

        def next_gen():
            i = gen_i[0] % 4
            gen_i[0] += 1
            return gen[i], ("gen", i)

        def next_pair():
            i = pair_i[0] % 2
            pair_i[0] += 1
            return pair[i], i

        ident = sb("ident", [128, 128])
        identb = sb("identb", [128, 128], BF16)
        mask = {d: sb("mask_" + d, [128, 128]) for d in "fb"}
        Lm = {d: sb("L_" + d, [128, 128], BF16) for d in "fb"}
        negcol = sb("negcol", [128, 1], BF16)
        band = sb("band", [128, 20, 128], BF16)
        cs_cos = sb("cs_cos", [128, NCK, 2, 32])
        cs_sin = sb("cs_sin", [128, NCK, 2, 32])
        wup32 = {d: sb("wup32_" + d, [65, 512]) for d in "fb"}
        wup = {d: sb("wupsb_" + d, [65, 512], BF16) for d in "fb"}
        modfm = sb("modfm", [128, 16, 2])
        bmod = sb("bmod", [128, 24])
        normg = sb("normg", [128, 8])
        gs = {s: sb("gs_" + s, [128, 8]) for s in "xc"}
        sh = {s: sb("sh_" + s, [128, 8]) for s in "xc"}
        glag = sb("glag", [128, 8])
        pscale = sb("pscale", [128, 8])
        fng = sb("fng", [128, D])
        gate_bc = sb("gate_bc", [128, D])
        wpool = sb("wpool", [128, 4, 2, 256], BF16)
        cc = sb("cc", [128, 8, 2])
        scc = sb("scc", [128, 8, 2])
        sccb = sb("sccb", [128, 8, 128])
        st = sb("st", [128, 16])
        D4 = {d: sb("D4_" + d, [128, 4]) for d in "fb"}

        xs = [sb("xs%d" % i, [128, D]) for i in range(2)]
        xr = [sb("xr%d" % i, [128, D]) for i in range(2)]
        junk = sb("junk", [128, D], BF16)
        hxT = sb("hxT", [128, 8, HR * 128], BF16)
        wr = [sb("wr%d" % i, [128, 8, 512], BF16) for i in range(3)]
        wlr = {d: sb("wlr_" + d, [128, 8, 16], BF16) for d in "fb"}
        qr = sb("qr", [128, NCH, 512])
        kr = sb("kr", [128, NCH, 512])
        vb = sb("vb", [128, NCH, D], BF16)
        sz = sb("sz", [128, NCH, D], BF16)
        lrT = {d: sb("lrT_" + d, [65, NCH * 128], BF16) for d in "fb"}
        sp_ = {d: sb("sp_" + d, [128, 512], BF16) for d in "fb"}
        xsb = [sb("xsb%d" % i, [128, D], BF16) for i in range(2)]
        eb = {d: sb("eb_" + d, [128, 512]) for d in "fb"}
        enb = {d: sb("enb_" + d, [128, 512]) for d in "fb"}
        qd = {d: sb("qd_" + d, [128, 512], BF16) for d in "fb"}
        kd = {d: sb("kd_" + d, [128, 512], BF16) for d in "fb"}
        qdT = {d: sb("qdT_" + d, [128, 4, 128], BF16) for d in "fb"}
        kdT = {d: sb("kdT_" + d, [128, 4, 128], BF16) for d in "fb"}
        sT = {d: sb("sT_" + d, [128, 4, 128], BF16) for d in "fb"}
        t1 = sb("t1", [128, 256])
        t2 = sb("t2", [128, 256])
        S = {d: sb("S_" + d, [128, D]) for d in "fb"}
        Sf_bf = sb("Sf_bf", [128, D], BF16)
        Sb_bf = [sb("Sb_bf%d" % i, [128, D], BF16) for i in range(2)]
        Atok = sb("Atok", [128, D], BF16)
        AT = sb("AT", [128, 8, NCH * 128], BF16)
        sgA = sb("sgA", [128, NCH, D], BF16)
        m1 = sb("m1", [128, NCH, D])
        ub = sb("ub", [128, UR, D], BF16)
        dT = sb("dT", [128, 8, 128], BF16)
        szB = sb("szB", [128, NCH, D], BF16)
        Btok = sb("Btok", [128, D], BF16)
        BT = sb("BT", [128, 8, NCH * 128], BF16)
        sgB = sb("sgB", [128, NCH, D], BF16)
        mg = sb("mg", [128, D], BF16)
        mT = sb("mT", [128, 8, NCH * 128], BF16)
        t3 = sb("t3", [128, 256])
        t4 = sb("t4", [128, 256])
        st2 = sb("st2", [128, 4])
        rstd_all = sb("rstd_all", [128, NCK + 2])
        dbg = {}

        small_loads = [(cc, cc_d, "cc"), (bmod, bmod_d, "bmod"), (normg, normg_d, "normg"),
                       (glag, glag_d, "glag"), (pscale, pscale_d, "pscale"), (fng, fng_d, "fng"),
                       (wup32["f"], wup_d["f"], "wup32f"), (wup32["b"], wup_d["b"], "wup32b")]
        for t, d_, k in small_loads:
            P.op("sp", "dma_start", out=t[:], in_=d_, W=[k], dma=k)

        P.op("pool", "memset", ident[:], 1.0, W=["ident"])
        P.op("pool", "affine_select", out=ident[:], in_=ident[:], pattern=[[-1, 128]],
             compare_op=ALU.is_equal, fill=0.0, base=0, channel_multiplier=1, R=["ident"], W=["ident"])
        P.op("dve", "tensor_copy", out=identb[:], in_=ident[:], R=["ident"], W=["identb"])
        for d in "fb":
            pat, cm = ([[1, 128]], -1) if d == "f" else ([[-1, 128]], 1)
            P.op("pool", "memset", mask[d][:], 1.0, W=[("mask", d)])
            P.op("pool", "affine_select", out=mask[d][:], in_=mask[d][:], pattern=pat,
                 compare_op=ALU.is_ge, fill=0.0, base=0, channel_multiplier=cm,
                 R=[("mask", d)], W=[("mask", d)])
            P.op("dve", "tensor_scalar", out=Lm[d][:], in0=mask[d][:], scalar1=-1.0 / 16, scalar2=None,
                 op0=ALU.mult, R=[("mask", d)], W=[("L", d)])
            P.op("pool", "memset", lrT[d][:], 0.0, W=[("lrT", d, i) for i in range(NCH)])
            P.op("pool", "memset", lrT[d][32:33, :], 1.0, W=[("lrT", d, i) for i in range(NCH)])
            P.op("pool", "memset", lrT[d][64:65, :], 1.0, W=[("lrT", d, i) for i in range(NCH)])
            wk32 = "wup32" + d
            P.op("dve", "tensor_copy", out=wup[d][:], in_=wup32[d][:], R=[wk32], W=["wup" + d])
            P.op("dve", "tensor_tensor", out=wup32[d][64:65, :], in0=wup32[d][64:65, :], in1=wup[d][64:65, :],
                 op=ALU.subtract, R=[wk32, "wup" + d], W=[wk32])
            P.op("dve", "tensor_copy", out=wup[d][64:65, :], in_=wup32[d][64:65, :], R=[wk32], W=["wup" + d])
        P.op("pool", "memset", negcol[:], -1.0 / 16, W=["negcol"])

        tidx_i = sb("tidx_i", [128, 128], I32)
        tidx = sb("tidx", [128, 128])
        bt = sb("bt", [128, 128])
        bt2 = sb("bt2", [128, 128])
        corr = sb("corr", [128, 128])
        P.op("pool", "iota", tidx_i[:], pattern=[[1, 128]], base=0, channel_multiplier=0, W=["tidx_i"])
        P.op("dve", "tensor_copy", out=tidx[:], in_=tidx_i[:], R=["tidx_i"], W=["tidx"])
        for g, w in enumerate(WINS):
            hw = w // 2
            for vi, off in ((0, 0), (1, -128), (2, 128)):
                P.op("pool", "memset", bt[:], 1.0 / w, W=["bt"])
                P.op("pool", "affine_select", out=bt[:], in_=bt[:], pattern=[[-1, 128]], compare_op=ALU.is_ge,
                     fill=0.0, base=hw + off, channel_multiplier=1, R=["bt"], W=["bt"])
                P.op("pool", "affine_select", out=bt[:], in_=bt[:], pattern=[[1, 128]], compare_op=ALU.is_ge,
                     fill=0.0, base=hw - 1 - off, channel_multiplier=-1, R=["bt"], W=["bt"])
                if vi == 0:
                    P.op("dve", "tensor_tensor", out=band[:, g * 5 + 0, :], in0=bt[:], in1=ident[:],
                         op=ALU.subtract, R=["bt", "ident"], W=[("band", g * 5 + 0)])
                    for vj, (s1, s2) in ((3, (1.0, float(hw))), (4, (-1.0, float(128 + hw)))):
                        P.op("dve", "tensor_scalar", out=corr[:], in0=tidx[:], scalar1=s1, scalar2=s2,
                             op0=ALU.mult, op1=ALU.add, R=["tidx"], W=["corr"])
                        P.op("dve", "tensor_scalar", out=corr[:], in0=corr[:], scalar1=float(w), scalar2=1.0 / w,
                             op0=ALU.min, op1=ALU.mult, R=["corr"], W=["corr"])
                        P.op("dve", "reciprocal", out=corr[:], in_=corr[:], R=["corr"], W=["corr"])
                        P.op("dve", "tensor_tensor", out=bt2[:], in0=bt[:], in1=corr[:], op=ALU.mult,
                             R=["bt", "corr"], W=["bt2"])
                        P.op("dve", "tensor_tensor", out=band[:, g * 5 + vj, :], in0=bt2[:], in1=ident[:],
                             op=ALU.subtract, R=["bt2", "ident"], W=[("band", g * 5 + vj)])
                else:
                    P.op("dve", "tensor_copy", out=band[:, g * 5 + vi, :], in_=bt[:], R=["bt"],
                         W=[("band", g * 5 + vi)])

        fi = sb("fi", [128, 32], I32)
        ff = sb("ff", [128, 32])
        freq = sb("freq", [128, 32])
        pi_ = sb("pi_", [128, 1], I32)
        pf = sb("pf", [128, 1])
        hi64 = sb("hi64", [128, 1])
        colp = sb("colp", [128, 1])
        ci2 = sb("ci2", [128, NCK], I32)
        rowv = sb("rowv", [128, NCK])
        P.op("pool", "iota", fi[:], pattern=[[1, 32]], base=0, channel_multiplier=0, W=["fi"])
        P.op("pool", "iota", pi_[:], pattern=[[0, 1]], base=0, channel_multiplier=1, W=["pi"])
        P.op("pool", "iota", ci2[:], pattern=[[2, NCK]], base=0, channel_multiplier=0, W=["ci2"])
        P.op("dve", "tensor_copy", out=ff[:], in_=fi[:], R=["fi"], W=["ff"])
        P.op("dve", "tensor_copy", out=pf[:], in_=pi_[:], R=["pi"], W=["pf"])
        P.op("dve", "tensor_copy", out=rowv[:], in_=ci2[:], R=["ci2"], W=["rowv"])
        P.op("act", "activation", out=freq[:], in_=ff[:], func=AF.Exp, scale=-math.log(10000.0) / 32,
             R=["ff"], W=["freq"])
        P.op("dve", "tensor_scalar", out=hi64[:], in0=pf[:], scalar1=64.0, scalar2=None, op0=ALU.is_ge,
             R=["pf"], W=["hi64"])
        P.op("dve", "scalar_tensor_tensor", out=colp[:], in0=hi64[:], scalar=-64.0, in1=pf[:],
             op0=ALU.mult, op1=ALU.add, R=["hi64", "pf"], W=["colp"])
        P.op("dve", "tensor_scalar", out=rowv[:], in0=rowv[:], scalar1=hi64[:, 0:1], scalar2=None,
             op0=ALU.add, R=["rowv", "hi64"], W=["rowv"])
        HC = NCK // 2
        TWO_PI = 2 * math.pi
        v4 = lambda t_, dt=None: (t_[:] if dt is None else t_[:].bitcast(dt)).rearrange("p (c t f) -> p c t f", c=HC, t=2)
        ang, ang2, kq, ki = v4(xs[0]), v4(xs[1]), v4(xr[0]), v4(xr[1], I32)
        for hh in range(2):
            csl = slice(hh * HC, (hh + 1) * HC)
            P.op("dve", "tensor_tensor", out=ang[:, :, 0, :], in0=rowv[:, csl].unsqueeze(2).to_broadcast([128, HC, 32]),
                 in1=freq[:, :].unsqueeze(1).to_broadcast([128, HC, 32]), op=ALU.mult,
                 R=["rowv", "freq"], W=[("xs", 0)])
            P.op("dve", "tensor_scalar", out=ang[:, :, 1, :], in0=freq[:, :].unsqueeze(1).to_broadcast([128, HC, 32]),
                 scalar1=colp[:, 0:1], scalar2=None, op0=ALU.mult, R=["freq", "colp", ("xs", 0)], W=[("xs", 0)])
            for (dst, shift) in ((cs_sin, 0.0), (cs_cos, math.pi / 2)):
                P.op("dve", "tensor_scalar", out=ang2, in0=ang, scalar1=shift, scalar2=None, op0=ALU.add,
                     R=[("xs", 0)], W=[("xs", 1)])
                P.op("dve", "tensor_scalar", out=kq, in0=ang2, scalar1=1.0 / TWO_PI, scalar2=None, op0=ALU.mult,
                     R=[("xs", 1)], W=[("xr", 0)])
                P.op("dve", "tensor_copy", out=ki, in_=kq, R=[("xr", 0)], W=[("xr", 1)])
                P.op("dve", "tensor_copy", out=kq, in_=ki, R=[("xr", 1)], W=[("xr", 0)])
                P.op("dve", "scalar_tensor_tensor", out=ang2, in0=kq, scalar=-TWO_PI, in1=ang2,
                     op0=ALU.mult, op1=ALU.add, R=[("xr", 0), ("xs", 1)], W=[("xs", 1)])
                P.op("dve", "tensor_scalar", out=ang2, in0=ang2, scalar1=math.pi, scalar2=-math.pi,
                     op0=ALU.min, op1=ALU.max, R=[("xs", 1)], W=[("xs", 1)])
                P.op("act", "activation", out=dst[:, csl, :, :], in_=ang2, func=AF.Sin, R=[("xs", 1)],
                     W=[("cs", id(dst))])
        cs_keys = [("cs", id(cs_sin)), ("cs", id(cs_cos))]

        P.op("act", "activation", out=scc[:], in_=cc[:], func=AF.Silu, R=["cc"], W=["scc"])
        modp, modk = next_gen()
        for nb in range(16):
            j = nb % 2
            P.op("sp", "dma_start", out=xr[j][:].rearrange("p (kc n) -> p kc n", kc=8),
                 in_=wmod_d[:, nb * 128:(nb + 1) * 128].rearrange("(kc p) n -> p kc n", p=128),
                 W=[("xr", j)], dma=("xr", j))
            for kc in range(8):
                P.op("pe", "matmul", modp[:, nb * 2:nb * 2 + 2], lhsT=xr[j][:, kc * 128:(kc + 1) * 128],
                     rhs=scc[:, kc, :], start=(kc == 0), stop=(kc == 7),
                     R=[("xr", j), "scc"], W=[modk])
        P.op("dve", "tensor_tensor", out=modfm[:], in0=modp[:, 0:32].rearrange("p (a b) -> p a b", b=2),
             in1=bmod[:, 0:16].unsqueeze(2).to_broadcast([128, 16, 2]), op=ALU.add,
             R=[modk, "bmod"], W=["modfm"])
        for si, s in enumerate("xc"):
            P.op("dve", "scalar_tensor_tensor", out=gs[s][:], in0=modfm[:, 8:16, si], scalar=1.0, in1=normg[:],
                 op0=ALU.add, op1=ALU.mult, R=["modfm", "normg"], W=[("gs", s)])
            P.op("dve", "tensor_copy", out=sh[s][:], in_=modfm[:, 0:8, si], R=["modfm"], W=[("sh", s)])
        FIRST = ("k", "v0", "v1")
        for g in FIRST:
            off = GOFF[g]
            P.op("pool", "dma_start", out=wins[g],
                 in_=win_d[:, off:off + 512].rearrange("(kc p) n -> p kc n", p=128),
                 W=[("wins", g)], dma=("wins", g))
        for d in "fb":
            P.op("pool", "dma_start", out=wlrs[d],
                 in_=win_d[:, LROFF[d]:LROFF[d] + 16].rearrange("(kc p) n -> p kc n", p=128),
                 W=[("wlrs", d)], dma=("wlrs", d))
        for d in "fb":
            P.op("sp", "dma_start", out=wlr[d][:], in_=wlrs[d], R=[("wlrs", d)], W=[("wlr", d)], dma=("wlr", d))
        def prep_g():
            for g, off in GOFF.items():
                if g in FIRST:
                    continue
                P.op("pool", "dma_start", out=wins[g],
                     in_=win_d[:, off:off + 512].rearrange("(kc p) n -> p kc n", p=128),
                     W=[("wins", g)], dma=("wins", g))
                yield
            P.op("pool", "dma_start", out=wpools,
                 in_=wpool_d.rearrange("g (kc p) n -> p g kc n", p=128), W=["wpools"], dma="wpools")
            P.op("sp", "dma_start", out=wpool[:], in_=wpools, R=["wpools"], W=["wpool"], dma="wpool")
            P.op("dve", "tensor_copy", out=sccb[:], in_=scc[:, :, 0:1].to_broadcast([128, 8, 128]), R=["scc"], W=["sccb"])
            gp, gpi = next_pair()
            for kc in range(8):
                j = kc % 2
                P.op("sp", "dma_start", out=m1[:, j, :], in_=wmod_d[kc * 128:(kc + 1) * 128, 2048:3072],
                     W=[("m1", j, 0), ("m1", j, 1)], dma=("m1st", j))
                for h in range(2):
                    P.op("pe", "matmul", gp[:, h * 512:(h + 1) * 512], lhsT=sccb[:, kc, :],
                         rhs=m1[:, j, h * 512:(h + 1) * 512], start=(kc == 0), stop=(kc == 7),
                         R=[("m1", j, 0), ("m1", j, 1), "sccb"], W=[("pair", gpi, h)])
            P.op("sp", "dma_start", out=gate_bc[:], in_=bgate_d, W=["gate_bc"], dma="gate_bc")
            P.op("dve", "tensor_tensor", out=gate_bc[:], in0=gp[:], in1=gate_bc[:], op=ALU.add,
                 R=[("pair", gpi, 0), ("pair", gpi, 1), "gate_bc"], W=["gate_bc"])
            yield
            stg = [(Atok, "AtokW"), (Btok, "Btok")]
            items = []
            for (src, dst, kind, key) in ((wa_d, was, "row_glag", "was"), (wb_d, wbs, "row_pscale", "wbs"),
                                          (wo_d, wos, "col_gate", "wos")):
                for kc in range(8):
                    items.append((src, dst, kind, key, kc))

            def issue_load(n):
                src, dst, kind, key, kc = items[n]
                j = n % 2
                P.op("sp", "dma_start", out=m1[:, j, :], in_=src[kc * 128:(kc + 1) * 128, :],
                     W=[("m1", j, 0), ("m1", j, 1)], dma=("m1st", j))
            issue_load(0)
            for n, (src, dst, kind, key, kc) in enumerate(items):
                if n + 1 < len(items):
                    issue_load(n + 1)
                j = n % 2
                mk = [("m1", j, 0), ("m1", j, 1)]
                ot, okey = stg[j]
                okeys = [okey] + ([("Atok", h_) for h_ in range(4)] if okey == "AtokW" else [])
                if kind == "col_gate":
                    P.op("dve", "tensor_tensor", out=ot[:], in0=m1[:, j, :], in1=gate_bc[:],
                         op=ALU.mult, R=mk + ["gate_bc"], W=okeys)
                else:
                    scv = glag if kind == "row_glag" else pscale
                    P.op("act", "activation", out=ot[:], in_=m1[:, j, :], func=AF.Copy,
                         scale=scv[:, kc:kc + 1], R=mk + ["glag", "pscale"], W=okeys)
                P.op("pool", "dma_start", out=dst[:, kc, :], in_=ot[:], R=okeys, W=[key], dma=("stgst", j))
                yield

        wr_i = [0]

        def load_w(src_ap, src_key):
            i = wr_i[0] % 3
            wr_i[0] += 1
            P.op("sp", "dma_start", out=wr[i][:], in_=src_ap, R=[src_key], W=[("wr", i)], dma=("wr", i))
            return wr[i], ("wr", i)

        xs_i = [0]

        SK = lambda d_: [("S", d_, h_) for h_ in range(4)]
        BS0 = {"hx": hxT, "hxk": "hxT", "kr": kr, "vb": vb, "vbk": "vb", "lrT": lrT, "lrk": {"f": "f", "b": "b"}}
        BS1 = {"hx": AT, "hxk": "AT", "kr": qr, "vb": sz, "vbk": "sz", "lrT": {"b": lrT["f"]}, "lrk": {"b": "f"}}

        def norm_chunk(src_rows, s, slot, bs=BS0, rcol=None, compute=True):
            j = xs_i[0] % 2
            xs_i[0] += 1
            P.op("sp", "dma_start", out=xs[j][:], in_=src_rows, W=[("xs", j)], dma=("xs", j))
            rk_ = ("rstd", rcol)
            if compute:
                P.op("act", "activation", out=junk[:], in_=xs[j][:], func=AF.Square, accum_out=st[:, 0:1],
                     R=[("xs", j)], W=[("junk", 0), ("junk", 1), ("junk", 2), ("junk", 3), "st0"])
                P.op("act", "activation", out=st[:, 1:2], in_=st[:, 0:1], func=AF.Ln, scale=1.0 / D, bias=EPS,
                     R=["st0"], W=["st1"])
                P.op("act", "activation", out=rstd_all[:, rcol:rcol + 1], in_=st[:, 1:2], func=AF.Exp, scale=-0.5,
                     R=["st1"], W=[rk_])
            P.op("act", "activation", out=xsb[j][:], in_=xs[j][:], func=AF.Copy, scale=rstd_all[:, rcol:rcol + 1],
                 R=[("xs", j), rk_], W=[("xsb", j)])
            tp, tpk = next_gen()
            tpv = tp[:].bitcast(BF16)
            for kc in range(8):
                P.op("pe", "transpose", out=tpv[:, kc * 128:(kc + 1) * 128], in_=xsb[j][:, kc * 128:(kc + 1) * 128],
                     identity=identb[:], R=[("xsb", j), "identb"], W=[tpk])
            for kc in range(8):
                P.op("dve", "tensor_scalar", out=bs["hx"][:, kc, slot * 128:(slot + 1) * 128],
                     in0=tpv[:, kc * 128:(kc + 1) * 128], scalar1=gs[s][:, kc:kc + 1], scalar2=sh[s][:, kc:kc + 1],
                     op0=ALU.mult, op1=ALU.add, R=[tpk, ("gs", s), ("sh", s)],
                     W=[(bs["hxk"], slot, kc)])

        def run(g):
            for _ in g:
                pass

        def interleave(*gens, weights=None):
            active = list(gens)
            wts = {id(g): (weights[k] if weights else 1) for k, g in enumerate(gens)}
            while active:
                for g in list(active):
                    for _ in range(wts[id(g)]):
                        try:
                            next(g)
                        except StopIteration:
                            active.remove(g)
                            break

        def inproj_g(g, slots, evac, bs=BS0):
            wt, wk = load_w(wins[g], ("wins", g))
            for i, slot in enumerate(slots):
                pp, pk = next_gen()
                for kc in range(8):
                    P.op("pe", "matmul", pp[:], lhsT=bs["hx"][:, kc, slot * 128:(slot + 1) * 128], rhs=wt[:, kc, :],
                         start=(kc == 0), stop=(kc == 7), R=[(bs["hxk"], slot, kc), wk], W=[pk])
                evac(i, pp, pk)
                yield

        def inproj(g, slots, evac):
            run(inproj_g(g, slots, evac))

        def lrproj_g(dirs, slots, bs=BS0):
            for d in dirs:
                for i, slot in enumerate(slots):
                    pp, pk = next_gen()
                    for kc in range(8):
                        P.op("pe", "matmul", pp[0:16, 0:128], lhsT=wlr[d][:, kc, :],
                             rhs=bs["hx"][:, kc, slot * 128:(slot + 1) * 128], start=(kc == 0), stop=(kc == 7),
                             R=[(bs["hxk"], slot, kc), ("wlr", d)], W=[pk])
                    P.op("act", "activation", out=bs["lrT"][d][0:16, i * 128:(i + 1) * 128], in_=pp[0:16, 0:128],
                         func=AF.Copy, R=[pk], W=[("lrT", bs["lrk"][d], i)])
                    yield

        def lrproj(dirs, slots):
            run(lrproj_g(dirs, slots))

        def rope(dst, i, c, pp, pk):
            pv = pp[:].rearrange("p (h t a f) -> p h t a f", h=4, t=2, a=2)
            dv = dst[:, i, :].rearrange("p (h t a f) -> p h t a f", h=4, t=2, a=2)
            cosb = cs_cos[:, c, :, :].unsqueeze(1).to_broadcast([128, 4, 2, 32])
            sinb = cs_sin[:, c, :, :].unsqueeze(1).to_broadcast([128, 4, 2, 32])
            tv = [t_[:].rearrange("p (h t f) -> p h t f", h=4, t=2) for t_ in (t1, t2, t3, t4)]
            rk = [pk] + cs_keys
            ka, kb = (id(dst), i, "a"), (id(dst), i, "b")
            P.op("dve", "tensor_tensor", out=tv[0], in0=pv[:, :, :, 0, :], in1=cosb, op=ALU.mult, R=rk, W=["t1"])
            P.op("dve", "tensor_tensor", out=tv[1], in0=pv[:, :, :, 1, :], in1=sinb, op=ALU.mult, R=rk, W=["t2"])
            P.op("dve", "tensor_tensor", out=tv[2], in0=pv[:, :, :, 0, :], in1=sinb, op=ALU.mult, R=rk, W=["t3"])
            P.op("dve", "tensor_tensor", out=tv[3], in0=pv[:, :, :, 1, :], in1=cosb, op=ALU.mult, R=rk, W=["t4"])
            P.op("dve", "tensor_tensor", out=dv[:, :, :, 0, :], in0=tv[0], in1=tv[1], op=ALU.subtract,
                 R=["t1", "t2"], W=[ka])
            P.op("dve", "tensor_tensor", out=dv[:, :, :, 1, :], in0=tv[2], in1=tv[3], op=ALU.add,
                 R=["t3", "t4"], W=[kb])

        def decay_prep_g(d, i, need_eb, need_D, bs=BS0, tb=None):
            tb = tb or d
            gp_, gk_ = next_gen()
            P.op("pe", "matmul", gp_[:], lhsT=bs["lrT"][d][0:65, i * 128:(i + 1) * 128], rhs=wup[d][0:65, :],
                 start=True, stop=True, R=[("lrT", bs["lrk"][d], i), "wupf", "wupb"], W=[gk_])
            P.op("act", "activation", out=eb[tb][:], in_=gp_[:], func=AF.Exp, scale=-1.0, R=[gk_], W=[("eb", tb)])
            P.op("act", "activation", out=sp_[tb][:], in_=eb[tb][:], func=AF.Ln, bias=1.0,
                 R=[("eb", tb)], W=[("sp", tb)])
            yield
            bp, bk = next_gen()
            P.op("pe", "matmul", bp[:], lhsT=Lm[d][:], rhs=sp_[tb][:], start=True, stop=True,
                 R=[("L", d), ("sp", tb)], W=[bk])
            if need_D:
                dp, dk_ = next_gen()
                for h in range(4):
                    P.op("pe", "matmul", dp[:, h:h + 1], lhsT=sp_[tb][:, h * 128:(h + 1) * 128], rhs=negcol[:],
                         start=True, stop=True, R=[("sp", tb), "negcol"], W=[dk_])
            if need_eb:
                P.op("act", "activation", out=eb[tb][:], in_=bp[:], func=AF.Exp, R=[bk], W=[("eb", tb)])
            P.op("act", "activation", out=enb[tb][:], in_=bp[:], func=AF.Exp, scale=-1.0, R=[bk], W=[("enb", tb)])
            if need_D:
                P.op("act", "activation", out=D4[tb][:], in_=dp[:, 0:4], func=AF.Exp, R=[dk_], W=[("D4", tb)])
            yield

        def decay_prep(d, i, need_eb, need_D):
            run(decay_prep_g(d, i, need_eb, need_D))

        def state_update(d, i, vkey, bs=BS0, tb=None):
            tb = tb or d
            sp2, spi = next_pair()
            for h in range(4):
                P.op("pe", "matmul", sp2[:, h * 256:(h + 1) * 256], lhsT=kd[tb][:, h * 128:(h + 1) * 128],
                     rhs=bs["vb"][:, i, h * 256:(h + 1) * 256], start=True, stop=True,
                     R=[("kd", tb), (bs["vbk"], i, 0), (bs["vbk"], i, 1)], W=[("pair", spi, h // 2)])
            P.op("dve", "tensor_tensor", out=S[d][:], in0=S[d][:], in1=sp2[:], op=ALU.add,
                 R=SK(d) + [("pair", spi, 0), ("pair", spi, 1)], W=SK(d))
            P.op("dve", "tensor_tensor", out=S[d][:].rearrange("p (h v) -> p h v", h=4),
                 in0=S[d][:].rearrange("p (h v) -> p h v", h=4),
                 in1=D4[tb][:, :].unsqueeze(2).to_broadcast([128, 4, 256]), op=ALU.mult,
                 R=SK(d) + [("D4", tb)], W=SK(d))

        sb_i = [0]

        def save_Sb(c):
            j = sb_i[0] % 2
            sb_i[0] += 1
            P.op("act", "activation", out=Sb_bf[j][:], in_=S["b"][:], func=AF.Copy, R=SK("b"), W=[("Sb_bf", j)])
            P.op("pool", "dma_start", out=sbscr[c], in_=Sb_bf[j][:], R=[("Sb_bf", j)], W=[("sbscr", c)],
                 dma=("sbst", j))

        def kd_only(d, i, bs=BS0, tb=None):
            tb = tb or d
            P.op("dve", "tensor_tensor", out=kd[tb][:], in0=bs["kr"][:, i, :], in1=enb[tb][:], op=ALU.mult,
                 R=[(id(bs["kr"]), i, "a"), (id(bs["kr"]), i, "b"), ("enb", tb)], W=[("kd", tb)])

        def evac_v(half, bs=BS0):
            def f(i, pp, pk):
                P.op("act", "activation", out=bs["vb"][:, i, half * 512:(half + 1) * 512], in_=pp[:], func=AF.Copy,
                     R=[pk], W=[(bs["vbk"], i, half)])
            return f

        def evac_k_plain(i, pp, pk):
            P.op("dve", "tensor_copy", out=kr[:, i, :], in_=pp[:], R=[pk], W=[(id(kr), i, "a"), (id(kr), i, "b")])

        for d in "fb":
            P.op("pool", "memset", S[d][:], 0.0, W=SK(d))
        STAGE = stage

        for i in range(2 if STAGE >= 1 else 0):
            norm_chunk(ctx_d[i * 128:(i + 1) * 128, :], "c", i, rcol=NCK + i)
        for _ in range(1 if STAGE >= 1 else 0):
            inproj("k", [0, 1], evac_k_plain)
            inproj("v0", [0, 1], evac_v(0))
            inproj("v1", [0, 1], evac_v(1))
            lrproj("fb", [0, 1])
            for d, order in (("f", (0, 1)), ("b", (1, 0))):
                for i in order:
                    decay_prep(d, i, need_eb=False, need_D=True)
                    kd_only(d, i)
                    state_update(d, i, None)
            P.op("act", "activation", out=Sf_bf[:], in_=S["f"][:], func=AF.Copy, R=SK("f"), W=["Sf_bf"])
            save_Sb(NCK - 1)

        prep = prep_g()

        def A_load(t, bs):
            chunks = list(range(t * NCH, (t + 1) * NCH))
            slots = list(range(NCH))
            for i, c in enumerate(chunks):
                norm_chunk(x_d[c * 128:(c + 1) * 128, :], "x", i, bs, rcol=c)
                yield
            yield from inproj_g("k", slots, lambda i, pp, pk, cs=chunks: rope(bs["kr"], i, cs[i], pp, pk), bs)
            yield from inproj_g("v0", slots, evac_v(0, bs), bs)
            yield from inproj_g("v1", slots, evac_v(1, bs), bs)
            yield from lrproj_g("b", slots, bs)

        Ap = {"b": sgA[:].rearrange("p a b -> p (a b)").bitcast(F32),
              "f": sgB[:].rearrange("p a b -> p (a b)").bitcast(F32)}

        def A_chain(t, bs):
            chunks = list(range(t * NCH, (t + 1) * NCH))
            valid = [i for i in range(NCH - 1, -1, -1) if chunks[i] != 0]

            def chain_prep(i):
                tb = "bf"[i % 2]
                yield from decay_prep_g("b", i, False, True, bs, tb)
                kd_only("b", i, bs, tb)
                sp2, spi = next_pair()
                for h in range(4):
                    P.op("pe", "matmul", sp2[:, h * 256:(h + 1) * 256], lhsT=kd[tb][:, h * 128:(h + 1) * 128],
                         rhs=bs["vb"][:, i, h * 256:(h + 1) * 256], start=True, stop=True,
                         R=[("kd", tb), (bs["vbk"], i, 0), (bs["vbk"], i, 1)], W=[("pair", spi, h // 2)])
                P.op("dve", "tensor_tensor", out=Ap[tb].rearrange("p (h v) -> p h v", h=4),
                     in0=sp2[:].rearrange("p (h v) -> p h v", h=4),
                     in1=D4[tb][:, :].unsqueeze(2).to_broadcast([128, 4, 256]), op=ALU.mult,
                     R=[("pair", spi, 0), ("pair", spi, 1), ("D4", tb)], W=[("Ap", tb, h_) for h_ in range(4)])
                yield
            gens = [chain_prep(i) for i in valid]
            while gens:
                for g_ in list(gens):
                    try:
                        next(g_)
                    except StopIteration:
                        gens.remove(g_)
                yield

        def A_serial(t):
            chunks = list(range(t * NCH, (t + 1) * NCH))
            valid = [i for i in range(NCH - 1, -1, -1) if chunks[i] != 0]
            for i in valid:
                tb = "bf"[i % 2]
                for h in range(4):
                    hs = slice(h * 256, (h + 1) * 256)
                    P.op("dve", "scalar_tensor_tensor", out=S["b"][:, hs], in0=S["b"][:, hs],
                         scalar=D4[tb][:, h:h + 1], in1=Ap[tb][:, hs], op0=ALU.mult, op1=ALU.add,
                         R=[("S", "b", h), ("D4", tb), ("Ap", tb, h)], W=[("S", "b", h)])
                save_Sb(chunks[i] - 1)
                yield

        if STAGE >= 2:
            bsets = [BS0, BS1]
            run(A_load(NT - 1, bsets[(NT - 1) % 2]))
            import itertools
            pending = None
            for t in range(NT - 1, -1, -1):
                first = A_chain(t, bsets[t % 2])
                if pending is not None:
                    def _merge(ch=first, pend=pending):
                        try:
                            next(ch)
                        except StopIteration:
                            pass
                        yield
                        yield from pend
                        yield from ch
                    first = _merge()
                gens = [first]
                gens.append(A_load(t - 1, bsets[(t - 1) % 2]) if t > 0 else iter(()))
                gens.append(itertools.islice(prep, 2))
                interleave(*gens, weights=(1, 3, 1))
                pending = A_serial(t)

        if STAGE >= 2:
            run(pending)

        run(prep)
        out_evs = []
        xr_i = [0]

        def ckpt(n):
            if sub == n:
                raise _Stop()
        try:
          for t in (range(NT) if STAGE >= 4 else (range(1) if STAGE >= 3 else [])):
              chunks = list(range(t * NCH, (t + 1) * NCH))
              ncs = list(range(0, NCH + 1)) if t == 0 else [c for c in range(t * NCH + 1, (t + 1) * NCH + 1) if c < NCK]
              for c in ncs:
                  norm_chunk(x_d[c * 128:(c + 1) * 128, :], "x", c % HR, rcol=c, compute=False)
              slots = [c % HR for c in chunks]

              inproj("q", slots, lambda i, pp, pk, cs=chunks: rope(qr, i, cs[i], pp, pk))
              inproj("k", slots, lambda i, pp, pk, cs=chunks: rope(kr, i, cs[i], pp, pk))
              inproj("v0", slots, evac_v(0))
              inproj("v1", slots, evac_v(1))
              lrproj("fb", slots)

              def evac_act(dst, func, half, name):
                  def f(i, pp, pk):
                      P.op("act", "activation", out=dst[:, i, half * 512:(half + 1) * 512], in_=pp[:], func=func,
                           R=[pk], W=[(name, i, half)])
                  return f
              inproj("zA0", slots, evac_act(sz, AF.Silu, 0, "sz"))
              inproj("zA1", slots, evac_act(sz, AF.Silu, 1, "sz"))

              ckpt(1)
              def gla_dir(d, i):
                  yield from decay_prep_g(d, i, need_eb=True, need_D=(d == "f"))
                  P.op("dve", "scalar_tensor_tensor", out=qd[d][:], in0=qr[:, i, :], scalar=QSCALE, in1=eb[d][:],
                       op0=ALU.mult, op1=ALU.mult, R=[(id(qr), i, "a"), (id(qr), i, "b"), ("eb", d)], W=[("qd", d)])
                  kd_only(d, i)
                  yield
                  tq_, tqk_ = next_gen()
                  tpb = tq_[:].bitcast(BF16)
                  for h in range(4):
                      P.op("pe", "transpose", out=tpb[:, h * 128:(h + 1) * 128], in_=qd[d][:, h * 128:(h + 1) * 128],
                           identity=identb[:], R=[("qd", d), "identb"], W=[tqk_])
                  for h in range(4):
                      P.op("pe", "transpose", out=tpb[:, 512 + h * 128:512 + (h + 1) * 128],
                           in_=kd[d][:, h * 128:(h + 1) * 128], identity=identb[:],
                           R=[("kd", d), "identb"], W=[tqk_])
                  P.op("act", "activation", out=qdT[d][:].rearrange("p h t -> p (h t)"), in_=tpb[:, 0:512],
                       func=AF.Copy, R=[tqk_], W=[("qdT", d)])
                  P.op("dve", "tensor_copy", out=kdT[d][:].rearrange("p h t -> p (h t)"), in_=tpb[:, 512:1024],
                       R=[tqk_], W=[("kdT", d)])
                  yield
                  scp, sck = next_gen()
                  for h in range(4):
                      P.op("pe", "matmul", scp[:, h * 128:(h + 1) * 128], lhsT=kdT[d][:, h, :], rhs=qdT[d][:, h, :],
                           start=True, stop=True, R=[("kdT", d), ("qdT", d)], W=[sck])
                  P.op("dve", "tensor_tensor", out=sT[d][:], in0=scp[:].rearrange("p (h t) -> p h t", h=4),
                       in1=mask[d][:, :].unsqueeze(1).to_broadcast([128, 4, 128]), op=ALU.mult,
                       R=[sck, ("mask", d)], W=[("sT", d)])
                  yield

              def gla_chunk(i, c):
                  j = c % 2
                  P.op("sp", "dma_start", out=Sb_bf[j][:], in_=sbscr[c], R=[("sbscr", c)], W=[("Sb_bf", j)],
                       dma=("Sb_bf", j))
                  gf, gb = gla_dir("f", i), gla_dir("b", i)
                  act_ = [gf, gb]
                  while act_:
                      for g_ in list(act_):
                          try:
                              next(g_)
                          except StopIteration:
                              act_.remove(g_)
                      yield
                  op_, opi = next_pair()
                  vk = [("vb", i, 0), ("vb", i, 1)]
                  for h in range(4):
                      osl = op_[:, h * 256:(h + 1) * 256]
                      ok = ("pair", opi, h // 2)
                      P.op("pe", "matmul", osl, lhsT=sT["f"][:, h, :], rhs=vb[:, i, h * 256:(h + 1) * 256],
                           start=True, stop=False, R=[("sT", "f")] + vk, W=[ok])
                      P.op("pe", "matmul", osl, lhsT=qdT["f"][:, h, :], rhs=Sf_bf[:, h * 256:(h + 1) * 256],
                           start=False, stop=False, R=[("qdT", "f"), "Sf_bf"], W=[ok])
                      P.op("pe", "matmul", osl, lhsT=sT["b"][:, h, :], rhs=vb[:, i, h * 256:(h + 1) * 256],
                           start=False, stop=False, R=[("sT", "b")] + vk, W=[ok])
                      P.op("pe", "matmul", osl, lhsT=qdT["b"][:, h, :], rhs=Sb_bf[j][:, h * 256:(h + 1) * 256],
                           start=False, stop=True, R=[("qdT", "b"), ("Sb_bf", j)], W=[ok])
                  oks = [("pair", opi, 0), ("pair", opi, 1)]
                  sp2, spi = next_pair()
                  for h in range(4):
                      P.op("pe", "matmul", sp2[:, h * 256:(h + 1) * 256], lhsT=kd["f"][:, h * 128:(h + 1) * 128],
                           rhs=vb[:, i, h * 256:(h + 1) * 256], start=True, stop=True,
                           R=[("kd", "f")] + vk, W=[("pair", spi, h // 2)])
                  for h in range(4):
                      P.op("act", "activation", out=junk[:, h * 256:(h + 1) * 256], in_=op_[:, h * 256:(h + 1) * 256],
                           func=AF.Square, accum_out=st[:, 4 + h:5 + h], R=oks, W=[("junk", h), ("ss4", h)])
                  P.op("act", "activation", out=st[:, 8:12], in_=st[:, 4:8], func=AF.Ln, scale=1.0 / 256, bias=EPS,
                       R=[("ss4", h) for h in range(4)], W=["ln4"])
                  P.op("act", "activation", out=st[:, 12:16], in_=st[:, 8:12], func=AF.Exp, scale=-0.5,
                       R=["ln4"], W=["rstd4"])
                  P.op("dve", "tensor_tensor", out=S["f"][:], in0=S["f"][:], in1=sp2[:], op=ALU.add,
                       R=SK("f") + [("pair", spi, 0), ("pair", spi, 1)], W=SK("f"))
                  P.op("dve", "tensor_tensor", out=S["f"][:].rearrange("p (h v) -> p h v", h=4),
                       in0=S["f"][:].rearrange("p (h v) -> p h v", h=4),
                       in1=D4["f"][:, :].unsqueeze(2).to_broadcast([128, 4, 256]), op=ALU.mult,
                       R=SK("f") + [("D4", "f")], W=SK("f"))
                  for h in range(4):
                      P.op("dve", "scalar_tensor_tensor", out=Atok[:, h * 256:(h + 1) * 256],
                           in0=op_[:, h * 256:(h + 1) * 256], scalar=st[:, 12 + h:13 + h],
                           in1=sz[:, i, h * 256:(h + 1) * 256], op0=ALU.mult, op1=ALU.mult,
                           R=oks + ["rstd4", ("sz", i, 0), ("sz", i, 1)], W=[("Atok", h)])
                  P.op("act", "activation", out=Sf_bf[:], in_=S["f"][:], func=AF.Copy, R=SK("f"), W=["Sf_bf"])
                  yield
                  tg_, tgk_ = next_gen()
                  tgv_ = tg_[:].bitcast(BF16)
                  for kc in range(8):
                      P.op("pe", "transpose", out=tgv_[:, kc * 128:(kc + 1) * 128], in_=Atok[:, kc * 128:(kc + 1) * 128],
                           identity=identb[:], R=[("Atok", kc // 2), "identb"], W=[tgk_])
                  P.op("dve", "tensor_copy", out=AT[:, :, i * 128:(i + 1) * 128],
                       in_=tgv_.rearrange("p (k t) -> p k t", k=8), R=[tgk_],
                       W=[("AT", i)] + [("AT", i, kc_) for kc_ in range(8)])
                  yield

              def gla_all():
                  for i, c in enumerate(chunks):
                      yield from gla_chunk(i, c)

              ucs = list(range(0, NCH + 1)) if t == 0 else [c for c in range(t * NCH + 1, (t + 1) * NCH + 1) if c < NCK]
              uslots = [c % HR for c in ucs]

              def evac_u(half, ucs=ucs):
                  def f(i, pp, pk):
                      us = ucs[i] % UR
                      P.op("act", "activation", out=ub[:, us, half * 512:(half + 1) * 512], in_=pp[:], func=AF.Copy,
                           R=[pk], W=[("ub", us, half)])
                  return f

              def pool_chunk(i, c):
                  dp, dpi = next_pair()
                  for cb in range(8):
                      g = cb // 2
                      half = cb // 4
                      terms = []
                      if c > 0:
                          terms.append(((c - 1) % UR, g * 5 + 1))
                      vcur = 3 if c == 0 else (4 if c == NCK - 1 else 0)
                      terms.append((c % UR, g * 5 + vcur))
                      if c < NCK - 1:
                          terms.append(((c + 1) % UR, g * 5 + 2))
                      for ti, (us, bi) in enumerate(terms):
                          P.op("pe", "matmul", dp[:, cb * 128:(cb + 1) * 128], lhsT=ub[:, us, cb * 128:(cb + 1) * 128],
                               rhs=band[:, bi, :], start=(ti == 0), stop=(ti == len(terms) - 1),
                               R=[("ub", us, half), ("band", bi)], W=[("pair", dpi, half)])
                  P.op("dve", "tensor_copy", out=dT[:].rearrange("p k t -> p (k t)"), in_=dp[:],
                       R=[("pair", dpi, 0), ("pair", dpi, 1)], W=["dT"])
                  yield
                  yp, ypi = next_pair()
                  for g in range(4):
                      for kc in range(2):
                          P.op("pe", "matmul", yp[:, g * 256:(g + 1) * 256], lhsT=dT[:, g * 2 + kc, :],
                               rhs=wpool[:, g, kc, :], start=(kc == 0), stop=(kc == 1),
                               R=["dT", "wpool"], W=[("pair", ypi, g // 2)])
                  P.op("dve", "tensor_tensor", out=Btok[:], in0=yp[:], in1=szB[:, i, :], op=ALU.mult,
                       R=[("pair", ypi, 0), ("pair", ypi, 1), ("szB", i, 0), ("szB", i, 1)], W=["Btok"])
                  yield
                  tg_, tgk_ = next_gen()
                  tgv_ = tg_[:].bitcast(BF16)
                  for kc in range(8):
                      P.op("pe", "transpose", out=tgv_[:, kc * 128:(kc + 1) * 128], in_=Btok[:, kc * 128:(kc + 1) * 128],
                           identity=identb[:], R=["Btok", "identb"], W=[tgk_])
                  P.op("dve", "tensor_copy", out=BT[:, :, i * 128:(i + 1) * 128],
                       in_=tgv_.rearrange("p (k t) -> p k t", k=8), R=[tgk_], W=[("BT", i)])
                  yield

              def filler():
                  yield from inproj_g("u0", uslots, evac_u(0))
                  yield from inproj_g("u1", uslots, evac_u(1))
                  yield from inproj_g("zB0", slots, evac_act(szB, AF.Silu, 0, "szB"))
                  yield from inproj_g("zB1", slots, evac_act(szB, AF.Silu, 1, "szB"))
                  yield from inproj_g("gA0", slots, evac_act(sgA, AF.Sigmoid, 0, "sgA"))
                  yield from inproj_g("gA1", slots, evac_act(sgA, AF.Sigmoid, 1, "sgA"))
                  yield from inproj_g("gB0", slots, evac_act(sgB, AF.Sigmoid, 0, "sgB"))
                  yield from inproj_g("gB1", slots, evac_act(sgB, AF.Sigmoid, 1, "sgB"))
                  for i, c in enumerate(chunks):
                      yield from pool_chunk(i, c)

              interleave(gla_all(), filler(), weights=(1, 3))

              ckpt(2)
              for half in range(2):
                  wt, wk = load_w(was[:, :, half * 512:(half + 1) * 512], "was")
                  for i in range(NCH):
                      pp, pk = next_gen()
                      for kc in range(8):
                          P.op("pe", "matmul", pp[:], lhsT=AT[:, kc, i * 128:(i + 1) * 128], rhs=wt[:, kc, :],
                               start=(kc == 0), stop=(kc == 7), R=[("AT", i), wk], W=[pk])
                      P.op("dve", "tensor_tensor", out=m1[:, i, half * 512:(half + 1) * 512], in0=pp[:],
                           in1=sgA[:, i, half * 512:(half + 1) * 512], op=ALU.mult,
                           R=[pk, ("sgA", i, half)], W=[("m1", i, half)])

              ckpt(4)
              wts = [load_w(wbs[:, :, half * 512:(half + 1) * 512], "wbs") for half in range(2)]
              for i in range(NCH):
                  for half in range(2):
                      wt, wk = wts[half]
                      pp, pk = next_gen()
                      for kc in range(8):
                          P.op("pe", "matmul", pp[:], lhsT=BT[:, kc, i * 128:(i + 1) * 128], rhs=wt[:, kc, :],
                               start=(kc == 0), stop=(kc == 7), R=[("BT", i), wk], W=[pk])
                      hs = slice(half * 512, (half + 1) * 512)
                      m2d = "fb"[half]
                      P.op("dve", "tensor_tensor", out=eb[m2d][:], in0=pp[:], in1=sgB[:, i, hs], op=ALU.mult,
                           R=[pk, ("sgB", i, half)], W=[("eb", m2d)])
                      P.op("dve", "tensor_tensor", out=mg[:, hs], in0=eb[m2d][:], in1=m1[:, i, hs], op=ALU.add,
                           R=[("eb", m2d), ("m1", i, half)], W=[("mg", half)])
                  tg_, tgk_ = next_gen()
                  tgv_ = tg_[:].bitcast(BF16)
                  for kc in range(8):
                      P.op("pe", "transpose", out=tgv_[:, kc * 128:(kc + 1) * 128], in_=mg[:, kc * 128:(kc + 1) * 128],
                           identity=identb[:], R=[("mg", kc // 4), "identb"], W=[tgk_])
                  P.op("dve", "tensor_copy", out=mT[:, :, i * 128:(i + 1) * 128],
                       in_=tgv_.rearrange("p (k t) -> p k t", k=8), R=[tgk_], W=[("mT", i)])

              ckpt(5)
              wts = [load_w(wos[:, :, half * 512:(half + 1) * 512], "wos") for half in range(2)]
              for i, c in enumerate(chunks):
                  j = xr_i[0] % 2
                  xr_i[0] += 1
                  P.op("sp", "dma_start", out=xr[j][:], in_=x_d[c * 128:(c + 1) * 128, :], W=[("xr", j)],
                       dma=("xr", j))
                  yp, ypi = next_pair()
                  for half in range(2):
                      wt, wk = wts[half]
                      for kc in range(8):
                          P.op("pe", "matmul", yp[:, half * 512:(half + 1) * 512], lhsT=mT[:, kc, i * 128:(i + 1) * 128],
                               rhs=wt[:, kc, :], start=(kc == 0), stop=(kc == 7), R=[("mT", i), wk],
                               W=[("pair", ypi, half)])
                  P.op("dve", "tensor_tensor", out=xr[j][:], in0=yp[:], in1=xr[j][:], op=ALU.add,
                       R=[("pair", ypi, 0), ("pair", ypi, 1), ("xr", j)], W=[("xr", j)])
                  P.op("act", "activation", out=junk[:], in_=xr[j][:], func=AF.Square, accum_out=st2[:, 0:1],
                       R=[("xr", j)], W=[("junk", 0), ("junk", 1), ("junk", 2), ("junk", 3), "st2_0"])
                  P.op("act", "activation", out=st2[:, 1:2], in_=st2[:, 0:1], func=AF.Ln, scale=1.0 / D, bias=EPS,
                       R=["st2_0"], W=["st2_1"])
                  P.op("act", "activation", out=st2[:, 2:3], in_=st2[:, 1:2], func=AF.Exp, scale=-0.5,
                       R=["st2_1"], W=["st2_2"])
                  P.op("dve", "scalar_tensor_tensor", out=xr[j][:], in0=xr[j][:], scalar=st2[:, 2:3], in1=fng[:],
                       op0=ALU.mult, op1=ALU.mult, R=[("xr", j), "st2_2", "fng"], W=[("xr", j)])
                  ev = P.op("pool", "dma_start", out=out_d[c * 128:(c + 1) * 128, :], in_=xr[j][:],
                            R=[("xr", j)], dma=("ost", j))
                  out_evs.append(ev)
        except _Stop:
            pass
        if debug:
            dbg = dict(gate_bc=gate_bc, gs_x=gs["x"], sh_x=sh["x"], gs_c=gs["c"], sh_c=sh["c"], S_f=S["f"], S_b=S["b"],
                       cs_cos=cs_cos, cs_sin=cs_sin, band=band, mask_f=mask["f"], L_b=Lm["b"], kr=kr, qr=qr,
                       D4_f=D4["f"], D4_b=D4["b"], sp_b=sp_["b"], enb_b=enb["b"], lrT_f=lrT["f"], lrT_b=lrT["b"],
                       hxT=hxT, vb=vb, m1=m1, AT=AT, BT=BT, mT=mT, sz=sz, ub=ub, Sf_bf=Sf_bf)
            for name, t in dbg.items():
                dd = nc.dram_tensor("dbg_" + name, list(t.shape), t.dtype, kind="ExternalOutput").ap()
                ev = P.op("sp", "dma_start", out=dd, in_=t[:], R=list(P.reg.keys()), dma=("dbg", name))
                out_evs.append(ev)
        print('sbuf bytes/partition', sb_total[0], 'ops', {e: len(v) for e, v in P.ops.items()})
        P.emit(final_waits=out_evs)
    return nc


_NC_CACHE = {}


def _fm(v):
    v = np.asarray(v, dtype=np.float32)
    return np.ascontiguousarray(v.reshape(-1, 128).T)


def _wup(w, b):
    o = np.zeros((65, 512), np.float32)
    o[0:16] = w
    o[32] = b
    o[64] = b
    return o


def kernel(x, c, ctx, c_ctx, w_mod, b_mod, norm_g, w_in, w_gate_up_f, b_gate_f, w_gate_up_b, b_gate_b,
           gla_norm_g, w_pool, pool_scale, w_branch_a, w_branch_b, w_out, final_norm_g):
    f = lambda a: np.ascontiguousarray(np.asarray(a, dtype=np.float32))
    x, c, ctx, c_ctx = f(x), f(c), f(ctx), f(c_ctx)
    if "nc" not in _NC_CACHE:
        _NC_CACHE["nc"] = build_nc()
    nc = _NC_CACHE["nc"]
    b_mod0 = f(b_mod)[0]
    shared = {
        "w_mod": f(w_mod)[0],
        "bmod_fm": _fm(b_mod0),
        "bgate_bc": np.ascontiguousarray(np.broadcast_to(b_mod0[2048:3072][None, :], (128, D))),
        "normg_fm": _fm(f(norm_g)[0]),
        "w_in": f(w_in)[0],
        "wup_f": _wup(f(w_gate_up_f)[0], f(b_gate_f)[0]),
        "wup_b": _wup(f(w_gate_up_b)[0], f(b_gate_b)[0]),
        "glag_fm": _fm(np.tile(f(gla_norm_g)[0], 4)),
        "w_pool": f(w_pool)[0],
        "pscale_fm": _fm(f(pool_scale)[0]),
        "w_a": f(w_branch_a)[0],
        "w_b": f(w_branch_b)[0],
        "w_o": f(w_out)[0],
        "fng_bc": np.ascontiguousarray(np.broadcast_to(f(final_norm_g)[None, :], (128, D))),
    }
    in_maps = []
    for b in range(8):
        m = dict(shared)
        m["x"] = x[b]
        m["ctx"] = ctx[b]
        m["cc"] = np.ascontiguousarray(np.stack([_fm(c[b]), _fm(c_ctx)], axis=-1))
        in_maps.append(m)
    res = run_bass_kernel_spmd(nc, in_maps, core_ids=list(range(8)))
    return np.stack([np.asarray(r["out"], dtype=np.float32) for r in res.results], axis=0)
```

```python
import contextlib
import math
import numpy as np
import concourse.bass as bass
import concourse.mybir as mybir
from concourse.bass_utils import run_bass_kernel_spmd

F32 = mybir.dt.float32
BF16 = mybir.dt.bfloat16
I32 = mybir.dt.int32
AF = mybir.ActivationFunctionType
ALU = mybir.AluOpType

D = 1024
L = 4096
LC = 256
C = 128
NCK = L // C
NCH = 2
NT = NCK // NCH
HR = NCH + 1
UR = NCH + 2
DIN = 7200
EPS = 1e-6
QSCALE = 128 ** -0.5
WINS = (2, 4, 8, 16)
GOFF = dict(q=0, k=512, v0=1024, v1=1536, zA0=2048, zA1=2560, u0=3104, u1=3616,
            zB0=4128, zB1=4640, gA0=5152, gA1=5664, gB0=6176, gB1=6688)
LROFF = dict(f=3072, b=3088)


class Prog:
    ENGS = ("pe", "dve", "act", "pool", "sp")

    def __init__(self, nc):
        self.nc = nc
        self.ops = {e: [] for e in self.ENGS}
        self.clock = {e: {} for e in self.ENGS}
        self.evclock = {}
        self.reg = {}
        self.dma_cnt = {}

    def _need(self, eng, ev, waits):
        key, val = ev
        if key == eng == "pe":
            return
        ck = self.clock[eng]
        if ck.get(key, 0) >= val:
            return
        waits.append(ev)
        if key != eng:
            for k, v in self.evclock.get(ev, {}).items():
                if ck.get(k, 0) < v:
                    ck[k] = v
        if ck.get(key, 0) < val:
            ck[key] = val

    @staticmethod
    def _psum(k):
        return isinstance(k, tuple) and k[0] in ("gen", "pair", "tpb")

    def op(self, eng, meth, *args, R=(), W=(), dma=None, **kw):
        nk = lambda k: ("tpb",) if (isinstance(k, tuple) and k[0] == "tpb") else k
        R = [nk(k) for k in R]
        W = [nk(k) for k in W]
        W = list(dict.fromkeys(W + [k for k in R if self._psum(k)]))
        R = [k for k in R if not self._psum(k)]
        waits = []
        idx = len(self.ops[eng])
        for r in R:
            st = self.reg.get(r)
            if st and st[0] is not None:
                self._need(eng, st[0], waits)
        for w in W:
            st = self.reg.get(w)
            if st:
                if st[0] is not None:
                    self._need(eng, st[0], waits)
                for ev in st[1]:
                    self._need(eng, ev, waits)
        if dma is None:
            myev = (eng, idx + 1)
            self.evclock[myev] = dict(self.clock[eng])
        else:
            n = self.dma_cnt.get(dma, 0) + 16
            self.dma_cnt[dma] = n
            myev = (("dma", dma), n)
            ck_ = dict(self.clock[eng])
            ck_.pop(eng, None)
            self.evclock[myev] = ck_
        self.ops[eng].append((waits, meth, args, kw, None if dma is None else myev))
        for r in R:
            st = self.reg.setdefault(r, [None, []])
            st[1].append(myev)
        for w in W:
            self.reg[w] = [myev, []]
        return myev

    def emit(self, final_waits=()):
        nc = self.nc
        with contextlib.ExitStack() as es:
            semh = {}
            for e in self.ENGS:
                semh[e] = es.enter_context(nc.semaphore("s_" + e))
            for i, k in enumerate(self.dma_cnt):
                semh[("dma", k)] = es.enter_context(nc.semaphore("d%d" % i))
            block = es.enter_context(nc.Block())
            remap = {}
            for e in self.ENGS:
                c = 0
                for i, o in enumerate(self.ops[e]):
                    if o[4] is None:
                        c += 1
                    remap[(e, i + 1)] = c

            def val(k, v):
                return v if isinstance(k, tuple) else remap[(k, v)]

            def run(e, eng):
                for (waits, meth, args, kw, dmaev) in self.ops[e]:
                    for (k, v) in waits:
                        eng.wait_ge(semh[k], val(k, v))
                    ins = getattr(eng, meth)(*args, **kw)
                    if dmaev is None:
                        ins.then_inc(semh[e], 1)
                    else:
                        ins.then_inc(semh[dmaev[0]], 16)
                if e == "pool":
                    for (k, v) in final_waits:
                        eng.wait_ge(semh[k], val(k, v))

            block.tensor(lambda eng: run("pe", eng))
            block.vector(lambda eng: run("dve", eng))
            block.scalar(lambda eng: run("act", eng))
            block.gpsimd(lambda eng: run("pool", eng))
            block.sync(lambda eng: run("sp", eng))


class _Stop(Exception):
    pass


def build_nc(stage=4, debug=False, sub=99):
    nc = bass.Bass("TRN2", target_bir_lowering=False)
    P = Prog(nc)

    def din(name, shape, dt=F32):
        return nc.dram_tensor(name, list(shape), dt, kind="ExternalInput").ap()

    def dint(name, shape, dt):
        return nc.dram_tensor(name, list(shape), dt, kind="Internal").ap()

    x_d = din("x", [L, D])
    ctx_d = din("ctx", [LC, D])
    cc_d = din("cc", [128, 8, 2])
    wmod_d = din("w_mod", [D, 3 * D])
    bmod_d = din("bmod_fm", [128, 24])
    bgate_d = din("bgate_bc", [128, D])
    normg_d = din("normg_fm", [128, 8])
    win_d = din("w_in", [D, DIN])
    wup_d = {"f": din("wup_f", [65, 512]), "b": din("wup_b", [65, 512])}
    glag_d = din("glag_fm", [128, 8])
    wpool_d = din("w_pool", [4, 256, 256])
    pscale_d = din("pscale_fm", [128, 8])
    wa_d = din("w_a", [D, D])
    wb_d = din("w_b", [D, D])
    wo_d = din("w_o", [D, D])
    fng_d = din("fng_bc", [128, D])
    out_d = nc.dram_tensor("out", [L, D], F32, kind="ExternalOutput").ap()

    wins = {g: dint("wins_" + g, [128, 8, 512], BF16) for g in GOFF}
    wlrs = {d: dint("wlrs_" + d, [128, 8, 16], BF16) for d in "fb"}
    wpools = dint("wpools", [128, 4, 2, 256], BF16)
    was = dint("was", [128, 8, D], BF16)
    wbs = dint("wbs", [128, 8, D], BF16)
    wos = dint("wos", [128, 8, D], BF16)
    sbscr = dint("sbscr", [NCK, 128, D], BF16)

    es = contextlib.ExitStack()
    with es:
        sb_total = [0]

        def sb(name, shape, dt=F32):
            n = 1
            for v_ in shape[1:]:
                n *= v_
            sb_total[0] += n * (2 if dt == BF16 else 4)
            return es.enter_context(nc.sbuf_tensor("sb_" + name, list(shape), dt))

        def ps(name, shape, dt=F32):
            return es.enter_context(nc.psum_tensor("ps_" + name, list(shape), dt))

        gen = [ps("gen%d" % i, [128, 512]) for i in range(4)]
        pair = [ps("pair%d" % i, [128, 1024]) for i in range(2)]
        gen_i = [0]
        pair_i = [0]

        def next_gen():
            i = gen_i[0] % 4
            gen_i[0] += 1
            return gen[i], ("gen", i)

        def next_pair():
            i = pair_i[0] % 2
            pair_i[0] += 1
            return pair[i], i

        ident = sb("ident", [128, 128])
        identb = sb("identb", [128, 128], BF16)
        mask = {d: sb("mask_" + d, [128, 128]) for d in "fb"}
        Lm = {d: sb("L_" + d, [128, 128], BF16) for d in "fb"}
        negcol = sb("negcol", [128, 1], BF16)
        band = sb("band", [128, 20, 128], BF16)
        cs_cos = sb("cs_cos", [128, NCK, 2, 32])
        cs_sin = sb("cs_sin", [128, NCK, 2, 32])
        wup32 = {d: sb("wup32_" + d, [65, 512]) for d in "fb"}
        wup = {d: sb("wupsb_" + d, [65, 512], BF16) for d in "fb"}
        modfm = sb("modfm", [128, 16, 2])
        bmod = sb("bmod", [128, 24])
        normg = sb("normg", [128, 8])
        gs = {s: sb("gs_" + s, [128, 8]) for s in "xc"}
        sh = {s: sb("sh_" + s, [128, 8]) for s in "xc"}
        glag = sb("glag", [128, 8])
        pscale = sb("pscale", [128, 8])
        fng = sb("fng", [128, D])
        gate_bc = sb("gate_bc", [128, D])
        wpool = sb("wpool", [128, 4, 2, 256], BF16)
        cc = sb("cc", [128, 8, 2])
        scc = sb("scc", [128, 8, 2])
        sccb = sb("sccb", [128, 8, 128])
        st = sb("st", [128, 16])
        D4 = {d: sb("D4_" + d, [128, 4]) for d in "fb"}

        xs = [sb("xs%d" % i, [128, D]) for i in range(2)]
        xr = [sb("xr%d" % i, [128, D]) for i in range(2)]
        junk = sb("junk", [128, D], BF16)
        hxT = sb("hxT", [128, 8, HR * 128], BF16)
        wr = [sb("wr%d" % i, [128, 8, 512], BF16) for i in range(3)]
        wlr = {d: sb("wlr_" + d, [128, 8, 16], BF16) for d in "fb"}
        qr = sb("qr", [128, NCH, 512])
        kr = sb("kr", [128, NCH, 512])
        vb = sb("vb", [128, NCH, D], BF16)
        sz = sb("sz", [128, NCH, D], BF16)
        lrT = {d: sb("lrT_" + d, [65, NCH * 128], BF16) for d in "fb"}
        sp_ = {d: sb("sp_" + d, [128, 512], BF16) for d in "fb"}
        xsb = [sb("xsb%d" % i, [128, D], BF16) for i in range(2)]
        eb = {d: sb("eb_" + d, [128, 512]) for d in "fb"}
        enb = {d: sb("enb_" + d, [128, 512]) for d in "fb"}
        qd = {d: sb("qd_" + d, [128, 512], BF16) for d in "fb"}
        kd = {d: sb("kd_" + d, [128, 512], BF16) for d in "fb"}
        qdT = {d: sb("qdT_" + d, [128, 4, 128], BF16) for d in "fb"}
        kdT = {d: sb("kdT_" + d, [128, 4, 128], BF16) for d in "fb"}
        sT = {d: sb("sT_" + d, [128, 4, 128], BF16) for d in "fb"}
        t1 = sb("t1", [128, 256])
        t2 = sb("t2", [128, 256])
        S = {d: sb("S_" + d, [128, D]) for d in "fb"}
        Sf_bf = sb("Sf_bf", [128, D], BF16)
        Sb_bf = [sb("Sb_bf%d" % i, [128, D], BF16) for i in range(2)]
        Atok = sb("Atok", [128, D], BF16)
        AT = sb("AT", [128, 8, NCH * 128], BF16)
        sgA = sb("sgA", [128, NCH, D], BF16)
        m1 = sb("m1", [128, NCH, D])
        ub = sb("ub", [128, UR, D], BF16)
        dT = sb("dT", [128, 8, 128], BF16)
        szB = sb("szB", [128, NCH, D], BF16)
        Btok = sb("Btok", [128, D], BF16)
        BT = sb("BT", [128, 8, NCH * 128], BF16)
        sgB = sb("sgB", [128, NCH, D], BF16)
        mg = sb("mg", [128, D], BF16)
        mT = sb("mT", [128, 8, NCH * 128], BF16)
        t3 = sb("t3", [128, 256])
        t4 = sb("t4", [128, 256])
        st2 = sb("st2", [128, 4])
        rstd_all = sb("rstd_all", [128, NCK + 2])
        dbg = {}

        small_loads = [(cc, cc_d, "cc"), (bmod, bmod_d, "bmod"), (normg, normg_d, "normg"),
                       (glag, glag_d, "glag"), (pscale, pscale_d, "pscale"), (fng, fng_d, "fng"),
                       (wup32["f"], wup_d["f"], "wup32f"), (wup32["b"], wup_d["b"], "wup32b")]
        for t, d_, k in small_loads:
            P.op("sp", "dma_start", out=t[:], in_=d_, W=[k], dma=k)

        P.op("pool", "memset", ident[:], 1.0, W=["ident"])
        P.op("pool", "affine_select", out=ident[:], in_=ident[:], pattern=[[-1, 128]],
             compare_op=ALU.is_equal, fill=0.0, base=0, channel_multiplier=1, R=["ident"], W=["ident"])
        P.op("dve", "tensor_copy", out=identb[:], in_=ident[:], R=["ident"], W=["identb"])
        for d in "fb":
            pat, cm = ([[1, 128]], -1) if d == "f" else ([[-1, 128]], 1)
            P.op("pool", "memset", mask[d][:], 1.0, W=[("mask", d)])
            P.op("pool", "affine_select", out=mask[d][:], in_=mask[d][:], pattern=pat,
                 compare_op=ALU.is_ge, fill=0.0, base=0, channel_multiplier=cm,
                 R=[("mask", d)], W=[("mask", d)])
            P.op("dve", "tensor_scalar", out=Lm[d][:], in0=mask[d][:], scalar1=-1.0 / 16, scalar2=None,
                 op0=ALU.mult, R=[("mask", d)], W=[("L", d)])
            P.op("pool", "memset", lrT[d][:], 0.0, W=[("lrT", d, i) for i in range(NCH)])
            P.op("pool", "memset", lrT[d][32:33, :], 1.0, W=[("lrT", d, i) for i in range(NCH)])
            P.op("pool", "memset", lrT[d][64:65, :], 1.0, W=[("lrT", d, i) for i in range(NCH)])
            wk32 = "wup32" + d
            P.op("dve", "tensor_copy", out=wup[d][:], in_=wup32[d][:], R=[wk32], W=["wup" + d])
            P.op("dve", "tensor_tensor", out=wup32[d][64:65, :], in0=wup32[d][64:65, :], in1=wup[d][64:65, :],
                 op=ALU.subtract, R=[wk32, "wup" + d], W=[wk32])
            P.op("dve", "tensor_copy", out=wup[d][64:65, :], in_=wup32[d][64:65, :], R=[wk32], W=["wup" + d])
        P.op("pool", "memset", negcol[:], -1.0 / 16, W=["negcol"])

        tidx_i = sb("tidx_i", [128, 128], I32)
        tidx = sb("tidx", [128, 128])
        bt = sb("bt", [128, 128])
        bt2 = sb("bt2", [128, 128])
        corr = sb("corr", [128, 128])
        P.op("pool", "iota", tidx_i[:], pattern=[[1, 128]], base=0, channel_multiplier=0, W=["tidx_i"])
        P.op("dve", "tensor_copy", out=tidx[:], in_=tidx_i[:], R=["tidx_i"], W=["tidx"])
        for g, w in enumerate(WINS):
            hw = w // 2
            for vi, off in ((0, 0), (1, -128), (2, 128)):
                P.op("pool", "memset", bt[:], 1.0 / w, W=["bt"])
                P.op("pool", "affine_select", out=bt[:], in_=bt[:], pattern=[[-1, 128]], compare_op=ALU.is_ge,
                     fill=0.0, base=hw + off, channel_multiplier=1, R=["bt"], W=["bt"])
                P.op("pool", "affine_select", out=bt[:], in_=bt[:], pattern=[[1, 128]], compare_op=ALU.is_ge,
                     fill=0.0, base=hw - 1 - off, channel_multiplier=-1, R=["bt"], W=["bt"])
                if vi == 0:
                    P.op("dve", "tensor_tensor", out=band[:, g * 5 + 0, :], in0=bt[:], in1=ident[:],
                         op=ALU.subtract, R=["bt", "ident"], W=[("band", g * 5 + 0)])
                    for vj, (s1, s2) in ((3, (1.0, float(hw))), (4, (-1.0, float(128 + hw)))):
                        P.op("dve", "tensor_scalar", out=corr[:], in0=tidx[:], scalar1=s1, scalar2=s2,
                             op0=ALU.mult, op1=ALU.add, R=["tidx"], W=["corr"])
                        P.op("dve", "tensor_scalar", out=corr[:], in0=corr[:], scalar1=float(w), scalar2=1.0 / w,
                             op0=ALU.min, op1=ALU.mult, R=["corr"], W=["corr"])
                        P.op("dve", "reciprocal", out=corr[:], in_=corr[:], R=["corr"], W=["corr"])
                        P.op("dve", "tensor_tensor", out=bt2[:], in0=bt[:], in1=corr[:], op=ALU.mult,
                             R=["bt", "corr"], W=["bt2"])
                        P.op("dve", "tensor_tensor", out=band[:, g * 5 + vj, :], in0=bt2[:], in1=ident[:],
                             op=ALU.subtract, R=["bt2", "ident"], W=[("band", g * 5 + vj)])
                else:
                    P.op("dve", "tensor_copy", out=band[:, g * 5 + vi, :], in_=bt[:], R=["bt"],
                         W=[("band", g * 5 + vi)])

        fi = sb("fi", [128, 32], I32)
        ff = sb("ff", [128, 32])
        freq = sb("freq", [128, 32])
        pi_ = sb("pi_", [128, 1], I32)
        pf = sb("pf", [128, 1])
        hi64 = sb("hi64", [128, 1])
        colp = sb("colp", [128, 1])
        ci2 = sb("ci2", [128, NCK], I32)
        rowv = sb("rowv", [128, NCK])
        P.op("pool", "iota", fi[:], pattern=[[1, 32]], base=0, channel_multiplier=0, W=["fi"])
        P.op("pool", "iota", pi_[:], pattern=[[0, 1]], base=0, channel_multiplier=1, W=["pi"])
        P.op("pool", "iota", ci2[:], pattern=[[2, NCK]], base=0, channel_multiplier=0, W=["ci2"])
        P.op("dve", "tensor_copy", out=ff[:], in_=fi[:], R=["fi"], W=["ff"])
        P.op("dve", "tensor_copy", out=pf[:], in_=pi_[:], R=["pi"], W=["pf"])
        P.op("dve", "tensor_copy", out=rowv[:], in_=ci2[:], R=["ci2"], W=["rowv"])
        P.op("act", "activation", out=freq[:], in_=ff[:], func=AF.Exp, scale=-math.log(10000.0) / 32,
             R=["ff"], W=["freq"])
        P.op("dve", "tensor_scalar", out=hi64[:], in0=pf[:], scalar1=64.0, scalar2=None, op0=ALU.is_ge,
             R=["pf"], W=["hi64"])
        P.op("dve", "scalar_tensor_tensor", out=colp[:], in0=hi64[:], scalar=-64.0, in1=pf[:],
             op0=ALU.mult, op1=ALU.add, R=["hi64", "pf"], W=["colp"])
        P.op("dve", "tensor_scalar", out=rowv[:], in0=rowv[:], scalar1=hi64[:, 0:1], scalar2=None,
             op0=ALU.add, R=["rowv", "hi64"], W=["rowv"])
        HC = NCK // 2
        TWO_PI = 2 * math.pi
        v4 = lambda t_, dt=None: (t_[:] if dt is None else t_[:].bitcast(dt)).rearrange("p (c t f) -> p c t f", c=HC, t=2)
        ang, ang2, kq, ki = v4(xs[0]), v4(xs[1]), v4(xr[0]), v4(xr[1], I32)
        for hh in range(2):
            csl = slice(hh * HC, (hh + 1) * HC)
            P.op("dve", "tensor_tensor", out=ang[:, :, 0, :], in0=rowv[:, csl].unsqueeze(2).to_broadcast([128, HC, 32]),
                 in1=freq[:, :].unsqueeze(1).to_broadcast([128, HC, 32]), op=ALU.mult,
                 R=["rowv", "freq"], W=[("xs", 0)])
            P.op("dve", "tensor_scalar", out=ang[:, :, 1, :], in0=freq[:, :].unsqueeze(1).to_broadcast([128, HC, 32]),
                 scalar1=colp[:, 0:1], scalar2=None, op0=ALU.mult, R=["freq", "colp", ("xs", 0)], W=[("xs", 0)])
            for (dst, shift) in ((cs_sin, 0.0), (cs_cos, math.pi / 2)):
                P.op("dve", "tensor_scalar", out=ang2, in0=ang, scalar1=shift, scalar2=None, op0=ALU.add,
                     R=[("xs", 0)], W=[("xs", 1)])
                P.op("dve", "tensor_scalar", out=kq, in0=ang2, scalar1=1.0 / TWO_PI, scalar2=None, op0=ALU.mult,
                     R=[("xs", 1)], W=[("xr", 0)])
                P.op("dve", "tensor_copy", out=ki, in_=kq, R=[("xr", 0)], W=[("xr", 1)])
                P.op("dve", "tensor_copy", out=kq, in_=ki, R=[("xr", 1)], W=[("xr", 0)])
                P.op("dve", "scalar_tensor_tensor", out=ang2, in0=kq, scalar=-TWO_PI, in1=ang2,
                     op0=ALU.mult, op1=ALU.add, R=[("xr", 0), ("xs", 1)], W=[("xs", 1)])
                P.op("dve", "tensor_scalar", out=ang2, in0=ang2, scalar1=math.pi, scalar2=-math.pi,
                     op0=ALU.min, op1=ALU.max, R=[("xs", 1)], W=[("xs", 1)])
                P.op("act", "activation", out=dst[:, csl, :, :], in_=ang2, func=AF.Sin, R=[("xs", 1)],
                     W=[("cs", id(dst))])
        cs_keys = [("cs", id(cs_sin)), ("cs", id(cs_cos))]

        P.op("act", "activation", out=scc[:], in_=cc[:], func=AF.Silu, R=["cc"], W=["scc"])
        modp, modk = next_gen()
        for nb in range(16):
            j = nb % 2
            P.op("sp", "dma_start", out=xr[j][:].rearrange("p (kc n) -> p kc n", kc=8),
                 in_=wmod_d[:, nb * 128:(nb + 1) * 128].rearrange("(kc p) n -> p kc n", p=128),
                 W=[("xr", j)], dma=("xr", j))
            for kc in range(8):
                P.op("pe", "matmul", modp[:, nb * 2:nb * 2 + 2], lhsT=xr[j][:, kc * 128:(kc + 1) * 128],
                     rhs=scc[:, kc, :], start=(kc == 0), stop=(kc == 7),
                     R=[("xr", j), "scc"], W=[modk])
        P.op("dve", "tensor_tensor", out=modfm[:], in0=modp[:, 0:32].rearrange("p (a b) -> p a b", b=2),
             in1=bmod[:, 0:16].unsqueeze(2).to_broadcast([128, 16, 2]), op=ALU.add,
             R=[modk, "bmod"], W=["modfm"])
        for si, s in enumerate("xc"):
            P.op("dve", "scalar_tensor_tensor", out=gs[s][:], in0=modfm[:, 8:16, si], scalar=1.0, in1=normg[:],
                 op0=ALU.add, op1=ALU.mult, R=["modfm", "normg"], W=[("gs", s)])
            P.op("dve", "tensor_copy", out=sh[s][:], in_=modfm[:, 0:8, si], R=["modfm"], W=[("sh", s)])
        FIRST = ("k", "v0", "v1")
        for g in FIRST:
            off = GOFF[g]
            P.op("pool", "dma_start", out=wins[g],
                 in_=win_d[:, off:off + 512].rearrange("(kc p) n -> p kc n", p=128),
                 W=[("wins", g)], dma=("wins", g))
        for d in "fb":
            P.op("pool", "dma_start", out=wlrs[d],
                 in_=win_d[:, LROFF[d]:LROFF[d] + 16].rearrange("(kc p) n -> p kc n", p=128),
                 W=[("wlrs", d)], dma=("wlrs", d))
        for d in "fb":
            P.op("sp", "dma_start", out=wlr[d][:], in_=wlrs[d], R=[("wlrs", d)], W=[("wlr", d)], dma=("wlr", d))
        def prep_g():
            for g, off in GOFF.items():
                if g in FIRST:
                    continue
                P.op("pool", "dma_start", out=wins[g],
                     in_=win_d[:, off:off + 512].rearrange("(kc p) n -> p kc n", p=128),
                     W=[("wins", g)], dma=("wins", g))
                yield
            P.op("pool", "dma_start", out=wpools,
                 in_=wpool_d.rearrange("g (kc p) n -> p g kc n", p=128), W=["wpools"], dma="wpools")
            P.op("sp", "dma_start", out=wpool[:], in_=wpools, R=["wpools"], W=["wpool"], dma="wpool")
            P.op("dve", "tensor_copy", out=sccb[:], in_=scc[:, :, 0:1].to_broadcast([128, 8, 128]), R=["scc"], W=["sccb"])
            gp, gpi = next_pair()
            for kc in range(8):
                j = kc % 2
                P.op("sp", "dma_start", out=m1[:, j, :], in_=wmod_d[kc * 128:(kc + 1) * 128, 2048:3072],
                     W=[("m1", j, 0), ("m1", j, 1)], dma=("m1st", j))
                for h in range(2):
                    P.op("pe", "matmul", gp[:, h * 512:(h + 1) * 512], lhsT=sccb[:, kc, :],
                         rhs=m1[:, j, h * 512:(h + 1) * 512], start=(kc == 0), stop=(kc == 7),
                         R=[("m1", j, 0), ("m1", j, 1), "sccb"], W=[("pair", gpi, h)])
            P.op("sp", "dma_start", out=gate_bc[:], in_=bgate_d, W=["gate_bc"], dma="gate_bc")
            P.op("dve", "tensor_tensor", out=gate_bc[:], in0=gp[:], in1=gate_bc[:], op=ALU.add,
                 R=[("pair", gpi, 0), ("pair", gpi, 1), "gate_bc"], W=["gate_bc"])
            yield
            stg = [(Atok, "AtokW"), (Btok, "Btok")]
            items = []
            for (src, dst, kind, key) in ((wa_d, was, "row_glag", "was"), (wb_d, wbs, "row_pscale", "wbs"),
                                          (wo_d, wos, "col_gate", "wos")):
                for kc in range(8):
                    items.append((src, dst, kind, key, kc))

            def issue_load(n):
                src, dst, kind, key, kc = items[n]
                j = n % 2
                P.op("sp", "dma_start", out=m1[:, j, :], in_=src[kc * 128:(kc + 1) * 128, :],
                     W=[("m1", j, 0), ("m1", j, 1)], dma=("m1st", j))
            issue_load(0)
            for n, (src, dst, kind, key, kc) in enumerate(items):
                if n + 1 < len(items):
                    issue_load(n + 1)
                j = n % 2
                mk = [("m1", j, 0), ("m1", j, 1)]
                ot, okey = stg[j]
                okeys = [okey] + ([("Atok", h_) for h_ in range(4)] if okey == "AtokW" else [])
                if kind == "col_gate":
                    P.op("dve", "tensor_tensor", out=ot[:], in0=m1[:, j, :], in1=gate_bc[:],
                         op=ALU.mult, R=mk + ["gate_bc"], W=okeys)
                else:
                    scv = glag if kind == "row_glag" else pscale
                    P.op("act", "activation", out=ot[:], in_=m1[:, j, :], func=AF.Copy,
                         scale=scv[:, kc:kc + 1], R=mk + ["glag", "pscale"], W=okeys)
                P.op("pool", "dma_start", out=dst[:, kc, :], in_=ot[:], R=okeys, W=[key], dma=("stgst", j))
                yield

        wr_i = [0]

        def load_w(src_ap, src_key):
            i = wr_i[0] % 3
            wr_i[0] += 1
            P.op("sp", "dma_start", out=wr[i][:], in_=src_ap, R=[src_key], W=[("wr", i)], dma=("wr", i))
            return wr[i], ("wr", i)

        xs_i = [0]

        SK = lambda d_: [("S", d_, h_) for h_ in range(4)]
        BS0 = {"hx": hxT, "hxk": "hxT", "kr": kr, "vb": vb, "vbk": "vb", "lrT": lrT, "lrk": {"f": "f", "b": "b"}}
        BS1 = {"hx": AT, "hxk": "AT", "kr": qr, "vb": sz, "vbk": "sz", "lrT": {"b": lrT["f"]}, "lrk": {"b": "f"}}

        def norm_chunk(src_rows, s, slot, bs=BS0, rcol=None, compute=True):
            j = xs_i[0] % 2
            xs_i[0] += 1
            P.op("sp", "dma_start", out=xs[j][:], in_=src_rows, W=[("xs", j)], dma=("xs", j))
            rk_ = ("rstd", rcol)
            if compute:
                P.op("act", "activation", out=junk[:], in_=xs[j][:], func=AF.Square, accum_out=st[:, 0:1],
                     R=[("xs", j)], W=[("junk", 0), ("junk", 1), ("junk", 2), ("junk", 3), "st0"])
                P.op("act", "activation", out=st[:, 1:2], in_=st[:, 0:1], func=AF.Ln, scale=1.0 / D, bias=EPS,
                     R=["st0"], W=["st1"])
                P.op("act", "activation", out=rstd_all[:, rcol:rcol + 1], in_=st[:, 1:2], func=AF.Exp, scale=-0.5,
                     R=["st1"], W=[rk_])
            P.op("act", "activation", out=xsb[j][:], in_=xs[j][:], func=AF.Copy, scale=rstd_all[:, rcol:rcol + 1],
                 R=[("xs", j), rk_], W=[("xsb", j)])
            tp, tpk = next_gen()
            tpv = tp[:].bitcast(BF16)
            for kc in range(8):
                P.op("pe", "transpose", out=tpv[:, kc * 128:(kc + 1) * 128], in_=xsb[j][:, kc * 128:(kc + 1) * 128],
                     identity=identb[:], R=[("xsb", j), "identb"], W=[tpk])
            for kc in range(8):
                P.op("dve", "tensor_scalar", out=bs["hx"][:, kc, slot * 128:(slot + 1) * 128],
                     in0=tpv[:, kc * 128:(kc + 1) * 128], scalar1=gs[s][:, kc:kc + 1], scalar2=sh[s][:, kc:kc + 1],
                     op0=ALU.mult, op1=ALU.add, R=[tpk, ("gs", s), ("sh", s)],
                     W=[(bs["hxk"], slot, kc)])

        def run(g):
            for _ in g:
                pass

        def interleave(*gens, weights=None):
            active = list(gens)
            wts = {id(g): (weights[k] if weights else 1) for k, g in enumerate(gens)}
            while active:
                for g in list(active):
                    for _ in range(wts[id(g)]):
                        try:
                            next(g)
                        except StopIteration:
                            active.remove(g)
                            break

        def inproj_g(g, slots, evac, bs=BS0):
            wt, wk = load_w(wins[g], ("wins", g))
            for i, slot in enumerate(slots):
                pp, pk = next_gen()
                for kc in range(8):
                    P.op("pe", "matmul", pp[:], lhsT=bs["hx"][:, kc, slot * 128:(slot + 1) * 128], rhs=wt[:, kc, :],
                         start=(kc == 0), stop=(kc == 7), R=[(bs["hxk"], slot, kc), wk], W=[pk])
                evac(i, pp, pk)
                yield

        def inproj(g, slots, evac):
            run(inproj_g(g, slots, evac))

        def lrproj_g(dirs, slots, bs=BS0):
            for d in dirs:
                for i, slot in enumerate(slots):
                    pp, pk = next_gen()
                    for kc in range(8):
                        P.op("pe", "matmul", pp[0:16, 0:128], lhsT=wlr[d][:, kc, :],
                             rhs=bs["hx"][:, kc, slot * 128:(slot + 1) * 128], start=(kc == 0), stop=(kc == 7),
                             R=[(bs["hxk"], slot, kc), ("wlr", d)], W=[pk])
                    P.op("act", "activation", out=bs["lrT"][d][0:16, i * 128:(i + 1) * 128], in_=pp[0:16, 0:128],
                         func=AF.Copy, R=[pk], W=[("lrT", bs["lrk"][d], i)])
                    yield

        def lrproj(dirs, slots):
            run(lrproj_g(dirs, slots))

        def rope(dst, i, c, pp, pk):
            pv = pp[:].rearrange("p (h t a f) -> p h t a f", h=4, t=2, a=2)
            dv = dst[:, i, :].rearrange("p (h t a f) -> p h t a f", h=4, t=2, a=2)
            cosb = cs_cos[:, c, :, :].unsqueeze(1).to_broadcast([128, 4, 2, 32])
            sinb = cs_sin[:, c, :, :].unsqueeze(1).to_broadcast([128, 4, 2, 32])
            tv = [t_[:].rearrange("p (h t f) -> p h t f", h=4, t=2) for t_ in (t1, t2, t3, t4)]
            rk = [pk] + cs_keys
            ka, kb = (id(dst), i, "a"), (id(dst), i, "b")
            P.op("dve", "tensor_tensor", out=tv[0], in0=pv[:, :, :, 0, :], in1=cosb, op=ALU.mult, R=rk, W=["t1"])
            P.op("dve", "tensor_tensor", out=tv[1], in0=pv[:, :, :, 1, :], in1=sinb, op=ALU.mult, R=rk, W=["t2"])
            P.op("dve", "tensor_tensor", out=tv[2], in0=pv[:, :, :, 0, :], in1=sinb, op=ALU.mult, R=rk, W=["t3"])
            P.op("dve", "tensor_tensor", out=tv[3], in0=pv[:, :, :, 1, :], in1=cosb, op=ALU.mult, R=rk, W=["t4"])
            P.op("dve", "tensor_tensor", out=dv[:, :, :, 0, :], in0=tv[0], in1=tv[1], op=ALU.subtract,
                 R=["t1", "t2"], W=[ka])
            P.op("dve", "tensor_tensor", out=dv[:, :, :, 1, :], in0=tv[2], in1=tv[3], op=ALU.add,
                 R=["t3", "t4"], W=[kb])

        def decay_prep_g(d, i, need_eb, need_D, bs=BS0, tb=None):
            tb = tb or d
            gp_, gk_ = next_gen()
            P.op("pe", "matmul", gp_[:], lhsT=bs["lrT"][d][0:65, i * 128:(i + 1) * 128], rhs=wup[d][0:65, :],
                 start=True, stop=True, R=[("lrT", bs["lrk"][d], i), "wupf", "wupb"], W=[gk_])
            P.op("act", "activation", out=eb[tb][:], in_=gp_[:], func=AF.Exp, scale=-1.0, R=[gk_], W=[("eb", tb)])
            P.op("act", "activation", out=sp_[tb][:], in_=eb[tb][:], func=AF.Ln, bias=1.0,
                 R=[("eb", tb)], W=[("sp", tb)])
            yield
            bp, bk = next_gen()
            P.op("pe", "matmul", bp[:], lhsT=Lm[d][:], rhs=sp_[tb][:], start=True, stop=True,
                 R=[("L", d), ("sp", tb)], W=[bk])
            if need_D:
                dp, dk_ = next_gen()
                for h in range(4):
                    P.op("pe", "matmul", dp[:, h:h + 1], lhsT=sp_[tb][:, h * 128:(h + 1) * 128], rhs=negcol[:],
                         start=True, stop=True, R=[("sp", tb), "negcol"], W=[dk_])
            if need_eb:
                P.op("act", "activation", out=eb[tb][:], in_=bp[:], func=AF.Exp, R=[bk], W=[("eb", tb)])
            P.op("act", "activation", out=enb[tb][:], in_=bp[:], func=AF.Exp, scale=-1.0, R=[bk], W=[("enb", tb)])
            if need_D:
                P.op("act", "activation", out=D4[tb][:], in_=dp[:, 0:4], func=AF.Exp, R=[dk_], W=[("D4", tb)])
            yield

        def decay_prep(d, i, need_eb, need_D):
            run(decay_prep_g(d, i, need_eb, need_D))

        def state_update(d, i, vkey, bs=BS0, tb=None):
            tb = tb or d
            sp2, spi = next_pair()
            for h in range(4):
                P.op("pe", "matmul", sp2[:, h * 256:(h + 1) * 256], lhsT=kd[tb][:, h * 128:(h + 1) * 128],
                     rhs=bs["vb"][:, i, h * 256:(h + 1) * 256], start=True, stop=True,
                     R=[("kd", tb), (bs["vbk"], i, 0), (bs["vbk"], i, 1)], W=[("pair", spi, h // 2)])
            P.op("dve", "tensor_tensor", out=S[d][:], in0=S[d][:], in1=sp2[:], op=ALU.add,
                 R=SK(d) + [("pair", spi, 0), ("pair", spi, 1)], W=SK(d))
            P.op("dve", "tensor_tensor", out=S[d][:].rearrange("p (h v) -> p h v", h=4),
                 in0=S[d][:].rearrange("p (h v) -> p h v", h=4),
                 in1=D4[tb][:, :].unsqueeze(2).to_broadcast([128, 4, 256]), op=ALU.mult,
                 R=SK(d) + [("D4", tb)], W=SK(d))

        sb_i = [0]

        def save_Sb(c):
            j = sb_i[0] % 2
            sb_i[0] += 1
            P.op("act", "activation", out=Sb_bf[j][:], in_=S["b"][:], func=AF.Copy, R=SK("b"), W=[("Sb_bf", j)])
            P.op("pool", "dma_start", out=sbscr[c], in_=Sb_bf[j][:], R=[("Sb_bf", j)], W=[("sbscr", c)],
                 dma=("sbst", j))

        def kd_only(d, i, bs=BS0, tb=None):
            tb = tb or d
            P.op("dve", "tensor_tensor", out=kd[tb][:], in0=bs["kr"][:, i, :], in1=enb[tb][:], op=ALU.mult,
                 R=[(id(bs["kr"]), i, "a"), (id(bs["kr"]), i, "b"), ("enb", tb)], W=[("kd", tb)])

        def evac_v(half, bs=BS0):
            def f(i, pp, pk):
                P.op("act", "activation", out=bs["vb"][:, i, half * 512:(half + 1) * 512], in_=pp[:], func=AF.Copy,
                     R=[pk], W=[(bs["vbk"], i, half)])
            return f

        def evac_k_plain(i, pp, pk):
            P.op("dve", "tensor_copy", out=kr[:, i, :], in_=pp[:], R=[pk], W=[(id(kr), i, "a"), (id(kr), i, "b")])

        for d in "fb":
            P.op("pool", "memset", S[d][:], 0.0, W=SK(d))
        STAGE = stage

        for i in range(2 if STAGE >= 1 else 0):
            norm_chunk(ctx_d[i * 128:(i + 1) * 128, :], "c", i, rcol=NCK + i)
        for _ in range(1 if STAGE >= 1 else 0):
            inproj("k", [0, 1], evac_k_plain)
            inproj("v0", [0, 1], evac_v(0))
            inproj("v1", [0, 1], evac_v(1))
            lrproj("fb", [0, 1])
            for d, order in (("f", (0, 1)), ("b", (1, 0))):
                for i in order:
                    decay_prep(d, i, need_eb=False, need_D=True)
                    kd_only(d, i)
                    state_update(d, i, None)
            P.op("act", "activation", out=Sf_bf[:], in_=S["f"][:], func=AF.Copy, R=SK("f"), W=["Sf_bf"])
            save_Sb(NCK - 1)

        prep = prep_g()

        def A_load(t, bs):
            chunks = list(range(t * NCH, (t + 1) * NCH))
            slots = list(range(NCH))
            for i, c in enumerate(chunks):
                norm_chunk(x_d[c * 128:(c + 1) * 128, :], "x", i, bs, rcol=c)
                yield
            yield from inproj_g("k", slots, lambda i, pp, pk, cs=chunks: rope(bs["kr"], i, cs[i], pp, pk), bs)
            yield from inproj_g("v0", slots, evac_v(0, bs), bs)
            yield from inproj_g("v1", slots, evac_v(1, bs), bs)
            yield from lrproj_g("b", slots, bs)

        Ap = {"b": sgA[:].rearrange("p a b -> p (a b)").bitcast(F32),
              "f": sgB[:].rearrange("p a b -> p (a b)").bitcast(F32)}

        def A_chain(t, bs):
            chunks = list(range(t * NCH, (t + 1) * NCH))
            valid = [i for i in range(NCH - 1, -1, -1) if chunks[i] != 0]

            def chain_prep(i):
                tb = "bf"[i % 2]
                yield from decay_prep_g("b", i, False, True, bs, tb)
                kd_only("b", i, bs, tb)
                sp2, spi = next_pair()
                for h in range(4):
                    P.op("pe", "matmul", sp2[:, h * 256:(h + 1) * 256], lhsT=kd[tb][:, h * 128:(h + 1) * 128],
                         rhs=bs["vb"][:, i, h * 256:(h + 1) * 256], start=True, stop=True,
                         R=[("kd", tb), (bs["vbk"], i, 0), (bs["vbk"], i, 1)], W=[("pair", spi, h // 2)])
                P.op("dve", "tensor_tensor", out=Ap[tb].rearrange("p (h v) -> p h v", h=4),
                     in0=sp2[:].rearrange("p (h v) -> p h v", h=4),
                     in1=D4[tb][:, :].unsqueeze(2).to_broadcast([128, 4, 256]), op=ALU.mult,
                     R=[("pair", spi, 0), ("pair", spi, 1), ("D4", tb)], W=[("Ap", tb, h_) for h_ in range(4)])
                yield
            gens = [chain_prep(i) for i in valid]
            while gens:
                for g_ in list(gens):
                    try:
                        next(g_)
                    except StopIteration:
                        gens.remove(g_)
                yield

        def A_serial(t):
            chunks = list(range(t * NCH, (t + 1) * NCH))
            valid = [i for i in range(NCH - 1, -1, -1) if chunks[i] != 0]
            for i in valid:
                tb = "bf"[i % 2]
                for h in range(4):
                    hs = slice(h * 256, (h + 1) * 256)
                    P.op("dve", "scalar_tensor_tensor", out=S["b"][:, hs], in0=S["b"][:, hs],
                         scalar=D4[tb][:, h:h + 1], in1=Ap[tb][:, hs], op0=ALU.mult, op1=ALU.add,
                         R=[("S", "b", h), ("D4", tb), ("Ap", tb, h)], W=[("S", "b", h)])
                save_Sb(chunks[i] - 1)
                yield

        if STAGE >= 2:
            bsets = [BS0, BS1]
            run(A_load(NT - 1, bsets[(NT - 1) % 2]))
            import itertools
            pending = None
            for t in range(NT - 1, -1, -1):
                first = A_chain(t, bsets[t % 2])
                if pending is not None:
                    def _merge(ch=first, pend=pending):
                        try:
                            next(ch)
                        except StopIteration:
                            pass
                        yield
                        yield from pend
                        yield from ch
                    first = _merge()
                gens = [first]
                gens.append(A_load(t - 1, bsets[(t - 1) % 2]) if t > 0 else iter(()))
                gens.append(itertools.islice(prep, 2))
                interleave(*gens, weights=(1, 4, 1))
                pending = A_serial(t)

        if STAGE >= 2:
            run(pending)

        run(prep)
        out_evs = []
        xr_i = [0]

        def ckpt(n):
            if sub == n:
                raise _Stop()
        try:
          for t in (range(NT) if STAGE >= 4 else (range(1) if STAGE >= 3 else [])):
              chunks = list(range(t * NCH, (t + 1) * NCH))
              ncs = list(range(0, NCH + 1)) if t == 0 else [c for c in range(t * NCH + 1, (t + 1) * NCH + 1) if c < NCK]
              for c in ncs:
                  norm_chunk(x_d[c * 128:(c + 1) * 128, :], "x", c % HR, rcol=c, compute=False)
              slots = [c % HR for c in chunks]

              inproj("q", slots, lambda i, pp, pk, cs=chunks: rope(qr, i, cs[i], pp, pk))
              inproj("k", slots, lambda i, pp, pk, cs=chunks: rope(kr, i, cs[i], pp, pk))
              inproj("v0", slots, evac_v(0))
              inproj("v1", slots, evac_v(1))
              lrproj("fb", slots)

              def evac_act(dst, func, half, name):
                  def f(i, pp, pk):
                      P.op("act", "activation", out=dst[:, i, half * 512:(half + 1) * 512], in_=pp[:], func=func,
                           R=[pk], W=[(name, i, half)])
                  return f
              inproj("zA0", slots, evac_act(sz, AF.Silu, 0, "sz"))
              inproj("zA1", slots, evac_act(sz, AF.Silu, 1, "sz"))

              ckpt(1)
              def gla_dir(d, i):
                  yield from decay_prep_g(d, i, need_eb=True, need_D=(d == "f"))
                  P.op("dve", "scalar_tensor_tensor", out=qd[d][:], in0=qr[:, i, :], scalar=QSCALE, in1=eb[d][:],
                       op0=ALU.mult, op1=ALU.mult, R=[(id(qr), i, "a"), (id(qr), i, "b"), ("eb", d)], W=[("qd", d)])
                  kd_only(d, i)
                  yield
                  tq_, tqk_ = next_gen()
                  tpb = tq_[:].bitcast(BF16)
                  for h in range(4):
                      P.op("pe", "transpose", out=tpb[:, h * 128:(h + 1) * 128], in_=qd[d][:, h * 128:(h + 1) * 128],
                           identity=identb[:], R=[("qd", d), "identb"], W=[tqk_])
                  for h in range(4):
                      P.op("pe", "transpose", out=tpb[:, 512 + h * 128:512 + (h + 1) * 128],
                           in_=kd[d][:, h * 128:(h + 1) * 128], identity=identb[:],
                           R=[("kd", d), "identb"], W=[tqk_])
                  P.op("act", "activation", out=qdT[d][:].rearrange("p h t -> p (h t)"), in_=tpb[:, 0:512],
                       func=AF.Copy, R=[tqk_], W=[("qdT", d)])
                  P.op("dve", "tensor_copy", out=kdT[d][:].rearrange("p h t -> p (h t)"), in_=tpb[:, 512:1024],
                       R=[tqk_], W=[("kdT", d)])
                  yield
                  scp, sck = next_gen()
                  for h in range(4):
                      P.op("pe", "matmul", scp[:, h * 128:(h + 1) * 128], lhsT=kdT[d][:, h, :], rhs=qdT[d][:, h, :],
                           start=True, stop=True, R=[("kdT", d), ("qdT", d)], W=[sck])
                  P.op("dve", "tensor_tensor", out=sT[d][:], in0=scp[:].rearrange("p (h t) -> p h t", h=4),
                       in1=mask[d][:, :].unsqueeze(1).to_broadcast([128, 4, 128]), op=ALU.mult,
                       R=[sck, ("mask", d)], W=[("sT", d)])
                  yield

              def gla_chunk(i, c):
                  j = c % 2
                  P.op("sp", "dma_start", out=Sb_bf[j][:], in_=sbscr[c], R=[("sbscr", c)], W=[("Sb_bf", j)],
                       dma=("Sb_bf", j))
                  gf, gb = gla_dir("f", i), gla_dir("b", i)
                  act_ = [gf, gb]
                  while act_:
                      for g_ in list(act_):
                          try:
                              next(g_)
                          except StopIteration:
                              act_.remove(g_)
                      yield
                  op_, opi = next_pair()
                  vk = [("vb", i, 0), ("vb", i, 1)]
                  for h in range(4):
                      osl = op_[:, h * 256:(h + 1) * 256]
                      ok = ("pair", opi, h // 2)
                      P.op("pe", "matmul", osl, lhsT=sT["f"][:, h, :], rhs=vb[:, i, h * 256:(h + 1) * 256],
                           start=True, stop=False, R=[("sT", "f")] + vk, W=[ok])
                      P.op("pe", "matmul", osl, lhsT=qdT["f"][:, h, :], rhs=Sf_bf[:, h * 256:(h + 1) * 256],
                           start=False, stop=False, R=[("qdT", "f"), "Sf_bf"], W=[ok])
                      P.op("pe", "matmul", osl, lhsT=sT["b"][:, h, :], rhs=vb[:, i, h * 256:(h + 1) * 256],
                           start=False, stop=False, R=[("sT", "b")] + vk, W=[ok])
                      P.op("pe", "matmul", osl, lhsT=qdT["b"][:, h, :], rhs=Sb_bf[j][:, h * 256:(h + 1) * 256],
                           start=False, stop=True, R=[("qdT", "b"), ("Sb_bf", j)], W=[ok])
                  oks = [("pair", opi, 0), ("pair", opi, 1)]
                  sp2, spi = next_pair()
                  for h in range(4):
                      P.op("pe", "matmul", sp2[:, h * 256:(h + 1) * 256], lhsT=kd["f"][:, h * 128:(h + 1) * 128],
                           rhs=vb[:, i, h * 256:(h + 1) * 256], start=True, stop=True,
                           R=[("kd", "f")] + vk, W=[("pair", spi, h // 2)])
                  for h in range(4):
                      P.op("act", "activation", out=junk[:, h * 256:(h + 1) * 256], in_=op_[:, h * 256:(h + 1) * 256],
                           func=AF.Square, accum_out=st[:, 4 + h:5 + h], R=oks, W=[("junk", h), ("ss4", h)])
                  P.op("act", "activation", out=st[:, 8:12], in_=st[:, 4:8], func=AF.Ln, scale=1.0 / 256, bias=EPS,
                       R=[("ss4", h) for h in range(4)], W=["ln4"])
                  P.op("act", "activation", out=st[:, 12:16], in_=st[:, 8:12], func=AF.Exp, scale=-0.5,
                       R=["ln4"], W=["rstd4"])
                  P.op("dve", "tensor_tensor", out=S["f"][:], in0=S["f"][:], in1=sp2[:], op=ALU.add,
                       R=SK("f") + [("pair", spi, 0), ("pair", spi, 1)], W=SK("f"))
                  P.op("dve", "tensor_tensor", out=S["f"][:].rearrange("p (h v) -> p h v", h=4),
                       in0=S["f"][:].rearrange("p (h v) -> p h v", h=4),
                       in1=D4["f"][:, :].unsqueeze(2).to_broadcast([128, 4, 256]), op=ALU.mult,
                       R=SK("f") + [("D4", "f")], W=SK("f"))
                  for h in range(4):
                      P.op("dve", "scalar_tensor_tensor", out=Atok[:, h * 256:(h + 1) * 256],
                           in0=op_[:, h * 256:(h + 1) * 256], scalar=st[:, 12 + h:13 + h],
                           in1=sz[:, i, h * 256:(h + 1) * 256], op0=ALU.mult, op1=ALU.mult,
                           R=oks + ["rstd4", ("sz", i, 0), ("sz", i, 1)], W=[("Atok", h)])
                  P.op("act", "activation", out=Sf_bf[:], in_=S["f"][:], func=AF.Copy, R=SK("f"), W=["Sf_bf"])
                  yield
                  tg_, tgk_ = next_gen()
                  tgv_ = tg_[:].bitcast(BF16)
                  for kc in range(8):
                      P.op("pe", "transpose", out=tgv_[:, kc * 128:(kc + 1) * 128], in_=Atok[:, kc * 128:(kc + 1) * 128],
                           identity=identb[:], R=[("Atok", kc // 2), "identb"], W=[tgk_])
                  P.op("dve", "tensor_copy", out=AT[:, :, i * 128:(i + 1) * 128],
                       in_=tgv_.rearrange("p (k t) -> p k t", k=8), R=[tgk_],
                       W=[("AT", i)] + [("AT", i, kc_) for kc_ in range(8)])
                  yield

              def gla_all():
                  for i, c in enumerate(chunks):
                      yield from gla_chunk(i, c)

              ucs = list(range(0, NCH + 1)) if t == 0 else [c for c in range(t * NCH + 1, (t + 1) * NCH + 1) if c < NCK]
              uslots = [c % HR for c in ucs]

              def evac_u(half, ucs=ucs):
                  def f(i, pp, pk):
                      us = ucs[i] % UR
                      P.op("act", "activation", out=ub[:, us, half * 512:(half + 1) * 512], in_=pp[:], func=AF.Copy,
                           R=[pk], W=[("ub", us, half)])
                  return f

              def pool_chunk(i, c):
                  dp, dpi = next_pair()
                  for cb in range(8):
                      g = cb // 2
                      half = cb // 4
                      terms = []
                      if c > 0:
                          terms.append(((c - 1) % UR, g * 5 + 1))
                      vcur = 3 if c == 0 else (4 if c == NCK - 1 else 0)
                      terms.append((c % UR, g * 5 + vcur))
                      if c < NCK - 1:
                          terms.append(((c + 1) % UR, g * 5 + 2))
                      for ti, (us, bi) in enumerate(terms):
                          P.op("pe", "matmul", dp[:, cb * 128:(cb + 1) * 128], lhsT=ub[:, us, cb * 128:(cb + 1) * 128],
                               rhs=band[:, bi, :], start=(ti == 0), stop=(ti == len(terms) - 1),
                               R=[("ub", us, half), ("band", bi)], W=[("pair", dpi, half)])
                  P.op("dve", "tensor_copy", out=dT[:].rearrange("p k t -> p (k t)"), in_=dp[:],
                       R=[("pair", dpi, 0), ("pair", dpi, 1)], W=["dT"])
                  yield
                  yp, ypi = next_pair()
                  for g in range(4):
                      for kc in range(2):
                          P.op("pe", "matmul", yp[:, g * 256:(g + 1) * 256], lhsT=dT[:, g * 2 + kc, :],
                               rhs=wpool[:, g, kc, :], start=(kc == 0), stop=(kc == 1),
                               R=["dT", "wpool"], W=[("pair", ypi, g // 2)])
                  P.op("dve", "tensor_tensor", out=Btok[:], in0=yp[:], in1=szB[:, i, :], op=ALU.mult,
                       R=[("pair", ypi, 0), ("pair", ypi, 1), ("szB", i, 0), ("szB", i, 1)], W=["Btok"])
                  yield
                  tg_, tgk_ = next_gen()
                  tgv_ = tg_[:].bitcast(BF16)
                  for kc in range(8):
                      P.op("pe", "transpose", out=tgv_[:, kc * 128:(kc + 1) * 128], in_=Btok[:, kc * 128:(kc + 1) * 128],
                           identity=identb[:], R=["Btok", "identb"], W=[tgk_])
                  P.op("dve", "tensor_copy", out=BT[:, :, i * 128:(i + 1) * 128],
                       in_=tgv_.rearrange("p (k t) -> p k t", k=8), R=[tgk_], W=[("BT", i)])
                  yield

              def filler():
                  yield from inproj_g("u0", uslots, evac_u(0))
                  yield from inproj_g("u1", uslots, evac_u(1))
                  yield from inproj_g("zB0", slots, evac_act(szB, AF.Silu, 0, "szB"))
                  yield from inproj_g("zB1", slots, evac_act(szB, AF.Silu, 1, "szB"))
                  yield from inproj_g("gA0", slots, evac_act(sgA, AF.Sigmoid, 0, "sgA"))
                  yield from inproj_g("gA1", slots, evac_act(sgA, AF.Sigmoid, 1, "sgA"))
                  yield from inproj_g("gB0", slots, evac_act(sgB, AF.Sigmoid, 0, "sgB"))
                  yield from inproj_g("gB1", slots, evac_act(sgB, AF.Sigmoid, 1, "sgB"))
                  for i, c in enumerate(chunks):
                      yield from pool_chunk(i, c)

              interleave(gla_all(), filler(), weights=(1, 2))

              ckpt(2)
              for half in range(2):
                  wt, wk = load_w(was[:, :, half * 512:(half + 1) * 512], "was")
                  for i in range(NCH):
                      pp, pk = next_gen()
                      for kc in range(8):
                          P.op("pe", "matmul", pp[:], lhsT=AT[:, kc, i * 128:(i + 1) * 128], rhs=wt[:, kc, :],
                               start=(kc == 0), stop=(kc == 7), R=[("AT", i), wk], W=[pk])
                      P.op("dve", "tensor_tensor", out=m1[:, i, half * 512:(half + 1) * 512], in0=pp[:],
                           in1=sgA[:, i, half * 512:(half + 1) * 512], op=ALU.mult,
                           R=[pk, ("sgA", i, half)], W=[("m1", i, half)])

              ckpt(4)
              wts = [load_w(wbs[:, :, half * 512:(half + 1) * 512], "wbs") for half in range(2)]
              for i in range(NCH):
                  for half in range(2):
                      wt, wk = wts[half]
                      pp, pk = next_gen()
                      for kc in range(8):
                          P.op("pe", "matmul", pp[:], lhsT=BT[:, kc, i * 128:(i + 1) * 128], rhs=wt[:, kc, :],
                               start=(kc == 0), stop=(kc == 7), R=[("BT", i), wk], W=[pk])
                      hs = slice(half * 512, (half + 1) * 512)
                      m2d = "fb"[half]
                      P.op("dve", "tensor_tensor", out=eb[m2d][:], in0=pp[:], in1=sgB[:, i, hs], op=ALU.mult,
                           R=[pk, ("sgB", i, half)], W=[("eb", m2d)])
                      P.op("dve", "tensor_tensor", out=mg[:, hs], in0=eb[m2d][:], in1=m1[:, i, hs], op=ALU.add,
                           R=[("eb", m2d), ("m1", i, half)], W=[("mg", half)])
                  tg_, tgk_ = next_gen()
                  tgv_ = tg_[:].bitcast(BF16)
                  for kc in range(8):
                      P.op("pe", "transpose", out=tgv_[:, kc * 128:(kc + 1) * 128], in_=mg[:, kc * 128:(kc + 1) * 128],
                           identity=identb[:], R=[("mg", kc // 4), "identb"], W=[tgk_])
                  P.op("dve", "tensor_copy", out=mT[:, :, i * 128:(i + 1) * 128],
                       in_=tgv_.rearrange("p (k t) -> p k t", k=8), R=[tgk_], W=[("mT", i)])

              ckpt(5)
              wts = [load_w(wos[:, :, half * 512:(half + 1) * 512], "wos") for half in range(2)]
              for i, c in enumerate(chunks):
                  j = xr_i[0] % 2
                  xr_i[0] += 1
                  P.op("sp", "dma_start", out=xr[j][:], in_=x_d[c * 128:(c + 1) * 128, :], W=[("xr", j)],
                       dma=("xr", j))
                  yp, ypi = next_pair()
                  for half in range(2):
                      wt, wk = wts[half]
                      for kc in range(8):
                          P.op("pe", "matmul", yp[:, half * 512:(half + 1) * 512], lhsT=mT[:, kc, i * 128:(i + 1) * 128],
                               rhs=wt[:, kc, :], start=(kc == 0), stop=(kc == 7), R=[("mT", i), wk],
                               W=[("pair", ypi, half)])
                  P.op("dve", "tensor_tensor", out=xr[j][:], in0=yp[:], in1=xr[j][:], op=ALU.add,
                       R=[("pair", ypi, 0), ("pair", ypi, 1), ("xr", j)], W=[("xr", j)])
                  P.op("act", "activation", out=junk[:], in_=xr[j][:], func=AF.Square, accum_out=st2[:, 0:1],
                       R=[("xr", j)], W=[("junk", 0), ("junk", 1), ("junk", 2), ("junk", 3), "st2_0"])
                  P.op("act", "activation", out=st2[:, 1:2], in_=st2[:, 0:1], func=AF.Ln, scale=1.0 / D, bias=EPS,
                       R=["st2_0"], W=["st2_1"])
                  P.op("act", "activation", out=st2[:, 2:3], in_=st2[:, 1:2], func=AF.Exp, scale=-0.5,
                       R=["st2_1"], W=["st2_2"])
                  P.op("dve", "scalar_tensor_tensor", out=xr[j][:], in0=xr[j][:], scalar=st2[:, 2:3], in1=fng[:],
                       op0=ALU.mult, op1=ALU.mult, R=[("xr", j), "st2_2", "fng"], W=[("xr", j)])
                  ev = P.op("pool", "dma_start", out=out_d[c * 128:(c + 1) * 128, :], in_=xr[j][:],
                            R=[("xr", j)], dma=("ost", j))
                  out_evs.append(ev)
        except _Stop:
            pass
        if debug:
            dbg = dict(gate_bc=gate_bc, gs_x=gs["x"], sh_x=sh["x"], gs_c=gs["c"], sh_c=sh["c"], S_f=S["f"], S_b=S["b"],
                       cs_cos=cs_cos, cs_sin=cs_sin, band=band, mask_f=mask["f"], L_b=Lm["b"], kr=kr, qr=qr,
                       D4_f=D4["f"], D4_b=D4["b"], sp_b=sp_["b"], enb_b=enb["b"], lrT_f=lrT["f"], lrT_b=lrT["b"],
                       hxT=hxT, vb=vb, m1=m1, AT=AT, BT=BT, mT=mT, sz=sz, ub=ub, Sf_bf=Sf_bf)
            for name, t in dbg.items():
                dd = nc.dram_tensor("dbg_" + name, list(t.shape), t.dtype, kind="ExternalOutput").ap()
                ev = P.op("sp", "dma_start", out=dd, in_=t[:], R=list(P.reg.keys()), dma=("dbg", name))
                out_evs.append(ev)
        print('sbuf bytes/partition', sb_total[0], 'ops', {e: len(v) for e, v in P.ops.items()})
        P.emit(final_waits=out_evs)
    return nc


_NC_CACHE = {}


def _fm(v):
    v = np.asarray(v, dtype=np.float32)
    return np.ascontiguousarray(v.reshape(-1, 128).T)


def _wup(w, b):
    o = np.zeros((65, 512), np.float32)
    o[0:16] = w
    o[32] = b
    o[64] = b
    return o


def kernel(x, c, ctx, c_ctx, w_mod, b_mod, norm_g, w_in, w_gate_up_f, b_gate_f, w_gate_up_b, b_gate_b,
           gla_norm_g, w_pool, pool_scale, w_branch_a, w_branch_b, w_out, final_norm_g):
    f = lambda a: np.ascontiguousarray(np.asarray(a, dtype=np.float32))
    x, c, ctx, c_ctx = f(x), f(c), f(ctx), f(c_ctx)
    if "nc" not in _NC_CACHE:
        _NC_CACHE["nc"] = build_nc()
    nc = _NC_CACHE["nc"]
    b_mod0 = f(b_mod)[0]
    shared = {
        "w_mod": f(w_mod)[0],
        "bmod_fm": _fm(b_mod0),
        "bgate_bc": np.ascontiguousarray(np.broadcast_to(b_mod0[2048:3072][None, :], (128, D))),
        "normg_fm": _fm(f(norm_g)[0]),
        "w_in": f(w_in)[0],
        "wup_f": _wup(f(w_gate_up_f)[0], f(b_gate_f)[0]),
        "wup_b": _wup(f(w_gate_up_b)[0], f(b_gate_b)[0]),
        "glag_fm": _fm(np.tile(f(gla_norm_g)[0], 4)),
        "w_pool": f(w_pool)[0],
        "pscale_fm": _fm(f(pool_scale)[0]),
        "w_a": f(w_branch_a)[0],
        "w_b": f(w_branch_b)[0],
        "w_o": f(w_out)[0],
        "fng_bc": np.ascontiguousarray(np.broadcast_to(f(final_norm_g)[None, :], (128, D))),
    }
    in_maps = []
    for b in range(8):
        m = dict(shared)
        m["x"] = x[b]
        m["ctx"] = ctx[b]
        m["cc"] = np.ascontiguousarray(np.stack([_fm(c[b]), _fm(c_ctx)], axis=-1))
        in_maps.append(m)
    res = run_bass_kernel_spmd(nc, in_maps, core_ids=list(range(8)))
    return np.stack([np.asarray(r["out"], dtype=np.float32) for r in res.results], axis=0)
```
